# Optimizing a Trainium2 kernel written in Bass

```python
import math
import jax, jax.numpy as jnp
from jax import lax
import numpy as np

D_MODEL = 2048
BATCH = 2
SEQ = 8192
DEPTH = 2
DEC_BATCH = 4
DEC_SEQ = 2048
PAST_LEN = 128

GRID_W = 64
HEAD_DIM = 128
NA_HEADS = 8
NA_WIN_ROWS = 8
NA_WIN_COLS = 16
MLA_HEADS = 8
MLA_Q_RANK = 512
MLA_KV_RANK = 512
MLA_NOPE = 128
MLA_ROPE = 64
MLA_V = 128
DIL_HEADS = 16
DIL_PAIRS = ((128, 1), (512, 4), (2048, 16))
DIL_HALF = 1024
Q_BLOCK = 128

WA = NA_HEADS * HEAD_DIM
WB = MLA_HEADS * MLA_V
WC = DIL_HEADS * HEAD_DIM
IN0_SIZES = (WA, WA, WA, WA, MLA_Q_RANK, MLA_KV_RANK, MLA_ROPE, WB)
IN0_WIDTH = 4 * WA + MLA_Q_RANK + MLA_KV_RANK + MLA_ROPE + WB
IN1_WIDTH = 4 * WC

ROPE_THETA = 10000.0
ALPHA = (2 * DEPTH) ** 0.25
BETA = (8 * DEPTH) ** -0.25
LN_EPS = 1e-5
RMS_EPS = 1e-6
NEG = -1e30

kernel_name = 'hybrid_na_mla_dilated_encoder'


def split_cols(t, sizes):
    idx, acc = [], 0
    for n in sizes[:-1]:
        acc += n
        idx.append(acc)
    return jnp.split(t, idx, axis=-1)


def layer_norm(x, g, b):
    xf = x.astype(jnp.float32)
    mu = xf.mean(-1, keepdims=True)
    var = jnp.square(xf - mu).mean(-1, keepdims=True)
    return ((xf - mu) * lax.rsqrt(var + LN_EPS) * g + b).astype(x.dtype)


def rms_norm(x, g):
    xf = x.astype(jnp.float32)
    return (xf * lax.rsqrt(jnp.square(xf).mean(-1, keepdims=True) + RMS_EPS) * g).astype(x.dtype)


def rope(x):
    s, d = x.shape[1], x.shape[-1]
    half = d // 2
    inv = ROPE_THETA ** (-jnp.arange(half, dtype=jnp.float32) / half)
    ang = jnp.arange(s, dtype=jnp.float32)[:, None] * inv[None, :]
    cos = jnp.cos(ang)[None, :, None, :]
    sin = jnp.sin(ang)[None, :, None, :]
    xf = x.astype(jnp.float32)
    x1, x2 = xf[..., :half], xf[..., half:]
    return jnp.concatenate([x1 * cos - x2 * sin, x2 * cos + x1 * sin], -1).astype(x.dtype)


def neighbourhood_attention(q, k, v, rpb):
    b, s, h, dh = q.shape
    rows = s // GRID_W
    kh = min(NA_WIN_ROWS, rows)
    kw = NA_WIN_COLS
    qg = q.reshape(b, rows, GRID_W, h, dh)
    kg = k.reshape(b, rows, GRID_W, h, dh)
    vg = v.reshape(b, rows, GRID_W, h, dh)
    col = jnp.arange(GRID_W)
    col_start = jnp.clip(col - kw // 2, 0, GRID_W - kw)
    col_idx = col_start[:, None] + jnp.arange(kw)[None, :]
    dc = col_idx - col[:, None] + (NA_WIN_COLS - 1)

    def one_row(r):
        r0 = jnp.clip(r - kh // 2, 0, rows - kh)
        qr = lax.dynamic_index_in_dim(qg, r, axis=1, keepdims=False)
        kr = lax.dynamic_slice_in_dim(kg, r0, kh, axis=1)
        vr = lax.dynamic_slice_in_dim(vg, r0, kh, axis=1)
        kn = kr[:, :, col_idx]
        vn = vr[:, :, col_idx]
        sc = jnp.einsum('bqhd,biqjhd->bhqij', qr, kn).astype(jnp.float32)
        dr = r0 + jnp.arange(kh) - r + (NA_WIN_ROWS - 1)
        bias = rpb[:, dr[:, None, None], dc[None, :, :]]
        sc = sc + bias.transpose(0, 2, 1, 3)[None].astype(jnp.float32)
        p = jax.nn.softmax(sc.reshape(b, h, GRID_W, kh * kw), axis=-1)
        p = p.reshape(b, h, GRID_W, kh, kw).astype(v.dtype)
        return jnp.einsum('bhqij,biqjhd->bqhd', p, vn)

    out = lax.map(one_row, jnp.arange(rows))
    return out.transpose(1, 0, 2, 3, 4).reshape(b, s, h * dh)


def dense_attention_blocks(q, k, v):
    b, s, h, dq = q.shape
    dv = v.shape[-1]
    nb = s // Q_BLOCK
    qb = q.reshape(b, nb, Q_BLOCK, h, dq).transpose(1, 0, 2, 3, 4)

    def attend(qi):
        sc = jnp.einsum('bqhd,bkhd->bhqk', qi, k).astype(jnp.float32)
        p = jax.nn.softmax(sc, axis=-1).astype(v.dtype)
        return jnp.einsum('bhqk,bkhd->bqhd', p, v)

    out = lax.map(attend, qb)
    return out.transpose(1, 0, 2, 3, 4).reshape(b, s, h * dv)


def mla_attention(q_lat, kv_lat, k_rope, q_norm_g, w_q_up, kv_norm_g, w_kv_up):
    b, s, _ = q_lat.shape
    q = (rms_norm(q_lat, q_norm_g) @ w_q_up).reshape(b, s, MLA_HEADS, MLA_NOPE + MLA_ROPE)
    q = jnp.concatenate([q[..., :MLA_NOPE], rope(q[..., MLA_NOPE:])], -1)
    kv = (rms_norm(kv_lat, kv_norm_g) @ w_kv_up).reshape(b, s, MLA_HEADS, MLA_NOPE + MLA_V)
    k_nope, v = kv[..., :MLA_NOPE], kv[..., MLA_NOPE:]
    k_pe = rope(k_rope[:, :, None, :])
    k = jnp.concatenate([k_nope, jnp.broadcast_to(k_pe, (b, s, MLA_HEADS, MLA_ROPE))], -1)
    return dense_attention_blocks(q * (MLA_NOPE + MLA_ROPE) ** -0.5, k, v)


def dilated_attention(q, k, v):
    b, s, h, dh = q.shape
    nb = s // Q_BLOCK
    pad = ((0, 0), (DIL_HALF, DIL_HALF), (0, 0), (0, 0))
    kp, vp = jnp.pad(k, pad), jnp.pad(v, pad)
    valid = jnp.pad(jnp.ones((s,), dtype=bool), (DIL_HALF, DIL_HALF))
    band_len = Q_BLOCK + 2 * DIL_HALF
    qb = q.reshape(b, nb, Q_BLOCK, h, dh).transpose(1, 0, 2, 3, 4)

    def one_block(args):
        i, qi = args
        s0 = i * Q_BLOCK
        kband = lax.dynamic_slice_in_dim(kp, s0, band_len, axis=1)
        vband = lax.dynamic_slice_in_dim(vp, s0, band_len, axis=1)
        mband = lax.dynamic_slice_in_dim(valid, s0, band_len)
        nums, dens, maxs = [], [], []
        for window, dil in DIL_PAIRS:
            half = window // 2
            n_side = half // dil
            lo = DIL_HALF - half
            ln = Q_BLOCK + 2 * half
            nq, nk = Q_BLOCK // dil, ln // dil
            kr = kband[:, lo:lo + ln].reshape(b, nk, dil, h, dh)
            vr = vband[:, lo:lo + ln].reshape(b, nk, dil, h, dh)
            mr = mband[lo:lo + ln].reshape(nk, dil)
            qr = qi.reshape(b, nq, dil, h, dh)
            sc = jnp.einsum('bqrhd,bkrhd->bhrqk', qr, kr).astype(jnp.float32)
            off = jnp.arange(nk)[None, :] - jnp.arange(nq)[:, None] - n_side
            mask = (jnp.abs(off) <= n_side)[None] & mr.T[:, None, :]
            sc = jnp.where(mask[None, None], sc, NEG)
            m = sc.max(-1, keepdims=True)
            e = jnp.exp(sc - m)
            den = e.sum(-1)
            num = jnp.einsum('bhrqk,bkrhd->bqrhd', e.astype(vr.dtype), vr).astype(jnp.float32)
            nums.append(num.reshape(b, Q_BLOCK, h, dh))
            dens.append(den.transpose(0, 3, 2, 1).reshape(b, Q_BLOCK, h))
            maxs.append(m[..., 0].transpose(0, 3, 2, 1).reshape(b, Q_BLOCK, h))
        ms = jnp.stack(maxs)
        w = jnp.exp(ms - ms.max(0))
        num_tot = (w[..., None] * jnp.stack(nums)).sum(0)
        den_tot = (w * jnp.stack(dens)).sum(0)
        return (num_tot / den_tot[..., None]).astype(q.dtype)

    out = lax.map(one_block, (jnp.arange(nb), qb))
    return out.transpose(1, 0, 2, 3, 4).reshape(b, s, h * dh)


def layer_ab(x, w_in, rpb, q_norm_g, w_q_up, kv_norm_g, w_kv_up, w_out, ln_g, ln_b):
    b, s, _ = x.shape
    qa, ka, va, ga, q_lat, kv_lat, k_rope, gb = split_cols(x @ w_in, IN0_SIZES)
    heads = lambda t: t.reshape(b, s, NA_HEADS, HEAD_DIM)
    ya = neighbourhood_attention(heads(qa) * HEAD_DIM ** -0.5, heads(ka), heads(va), rpb) * jax.nn.silu(ga)
    yb = mla_attention(q_lat, kv_lat, k_rope, q_norm_g, w_q_up, kv_norm_g, w_kv_up) * jax.nn.silu(gb)
    y = jnp.concatenate([ya, yb], -1) @ w_out
    return layer_norm(ALPHA * x + y, ln_g, ln_b)


def layer_c(x, w_in, w_out, ln_g, ln_b):
    b, s, _ = x.shape
    qc, kc, vc, gc = split_cols(x @ w_in, (WC, WC, WC, WC))
    heads = lambda t: t.reshape(b, s, DIL_HEADS, HEAD_DIM)
    q = rope(heads(qc)) * HEAD_DIM ** -0.5
    k = rope(heads(kc))
    y = (dilated_attention(q, k, heads(vc)) * jax.nn.silu(gc)) @ w_out
    return layer_norm(ALPHA * x + y, ln_g, ln_b)


def trunk(x, ab_params, c_params):
    for layer in range(DEPTH):
        if layer % 2 == 0:
            x = layer_ab(x, *ab_params)
        else:
            x = layer_c(x, *c_params)
    return x


def setup_inputs(seed: int = 0) -> dict:
    key = jax.random.key(seed)
    ks = jax.random.split(key, 16)
    f32 = jnp.float32
    nrm = lambda k, shape, scale: jax.random.normal(k, shape, f32) * scale
    return {
        'x_prompt': nrm(ks[0], (BATCH, SEQ, D_MODEL), 1.0),
        'x_sample': nrm(ks[1], (DEC_BATCH, DEC_SEQ, D_MODEL), 1.0),
        'ab_w_in': nrm(ks[2], (D_MODEL, IN0_WIDTH), D_MODEL ** -0.5),
        'ab_rpb': nrm(ks[3], (NA_HEADS, 2 * NA_WIN_ROWS - 1, 2 * NA_WIN_COLS - 1), 0.1),
        'ab_q_norm_g': 1.0 + nrm(ks[4], (MLA_Q_RANK,), 0.01),
        'ab_w_q_up': nrm(ks[5], (MLA_Q_RANK, MLA_HEADS * (MLA_NOPE + MLA_ROPE)), MLA_Q_RANK ** -0.5),
        'ab_kv_norm_g': 1.0 + nrm(ks[6], (MLA_KV_RANK,), 0.01),
        'ab_w_kv_up': nrm(ks[7], (MLA_KV_RANK, MLA_HEADS * (MLA_NOPE + MLA_V)), MLA_KV_RANK ** -0.5),
        'ab_w_out': nrm(ks[8], (WA + WB, D_MODEL), BETA * (WA + WB) ** -0.5),
        'ab_ln_g': 1.0 + nrm(ks[9], (D_MODEL,), 0.01),
        'ab_ln_b': nrm(ks[10], (D_MODEL,), 0.01),
        'c_w_in': nrm(ks[11], (D_MODEL, IN1_WIDTH), D_MODEL ** -0.5),
        'c_w_out': nrm(ks[12], (WC, D_MODEL), BETA * WC ** -0.5),
        'c_ln_g': 1.0 + nrm(ks[13], (D_MODEL,), 0.01),
        'c_ln_b': nrm(ks[14], (D_MODEL,), 0.01),
    }


def reference(x_prompt, x_sample, ab_w_in, ab_rpb, ab_q_norm_g, ab_w_q_up, ab_kv_norm_g, ab_w_kv_up,
              ab_w_out, ab_ln_g, ab_ln_b, c_w_in, c_w_out, c_ln_g, c_ln_b):
    ab_params = (ab_w_in, ab_rpb, ab_q_norm_g, ab_w_q_up, ab_kv_norm_g, ab_w_kv_up, ab_w_out, ab_ln_g, ab_ln_b)
    c_params = (c_w_in, c_w_out, c_ln_g, c_ln_b)
    y_prompt = trunk(x_prompt, ab_params, c_params)
    y_sample = trunk(x_sample, ab_params, c_params)
    return (y_prompt, y_sample)
```

```python
import contextlib
import numpy as np
import concourse.bass as bass
import concourse.mybir as mybir
from concourse.bass_utils import run_bass_kernel_spmd

F32, BF16, I32 = mybir.dt.float32, mybir.dt.bfloat16, mybir.dt.int32
ALU = mybir.AluOpType
AF = mybir.ActivationFunctionType

NCORES = 8
D = 2048
WT = 7168
ST = 10240
P_W0 = list(range(1, 9))
P_OUT = list(range(3, 7))
S_WT = list(range(10, 14))
W0_TILES = P_W0 + S_WT
SH_TILES = [14, 15]
OUT_TILES = P_OUT + SH_TILES
NOUT = len(OUT_TILES) * 512
WTX = WT + 1024
NEG = -30000.0
ALPHA = 4.0 ** 0.25
LN_EPS = 1e-5
RMS_EPS = 1e-6
SEM_ROT = 12000
ARENA_WORDS = 50176


class Buf:
    def __init__(self, name):
        self.name = name
        self.writes = {}
        self.reads = {}
        self.w_is_dma = False
        self.dsems = {}


def _merge(dst, src):
    for k, v in src.items():
        if dst.get(k, 0) < v:
            dst[k] = v


class Sched:
    ENG = ('pe', 'act', 'dve', 'pool', 'sp')

    def __init__(self, nc, stack):
        self.nc = nc
        self.stack = stack
        self.prog = {e: [] for e in self.ENG}
        self.ecount = {e: 0 for e in self.ENG}
        self.esems = {e: [] for e in self.ENG}
        self.waited = {e: {} for e in self.ENG}
        self.bufs = []
        self.nsem = 0
        self.semh = {}
        self.free_dsems = {}
        self.phase_bufs = []

    def newsem(self, name):
        self.nsem += 1
        h = self.stack.enter_context(self.nc.semaphore(f"{name}_{self.nsem}"))
        key = self.nsem
        self.semh[key] = h
        return key

    def buf(self, name):
        b = Buf(name)
        self.bufs.append(b)
        return b

    def _tok(self, e):
        k = self.ecount[e]
        idx = k // SEM_ROT
        while len(self.esems[e]) <= idx:
            self.esems[e].append(self.newsem(f"p{e}"))
        self.ecount[e] += 1
        return (self.esems[e][idx], k % SEM_ROT + 1)

    def _wait(self, e, need):
        for sk, val in need.items():
            if self.waited[e].get(sk, 0) >= val:
                continue
            self.waited[e][sk] = val
            h = self.semh[sk]
            self.prog[e].append(lambda E, h=h, val=val: E.wait_ge(h, val))

    def op(self, e, fn, reads=(), writes=()):
        need = {}
        for b in reads:
            _merge(need, b.writes)
        for b in writes:
            _merge(need, b.reads)
            _merge(need, b.writes)
        if e == 'pe':
            for sk in self.esems['pe']:
                need.pop(sk, None)
        self._wait(e, need)
        tok = self._tok(e)
        h = self.semh[tok[0]]
        self.prog[e].append(lambda E, fn=fn, h=h: fn(E).then_inc(h, 1))
        for b in reads:
            if b.reads.get(tok[0], 0) < tok[1]:
                b.reads[tok[0]] = tok[1]
        for b in writes:
            b.writes = {tok[0]: tok[1]}
            b.reads = {}
            b.w_is_dma = False

    def dma(self, q, fn, sb, reads=(), writes=()):
        need = {}
        for b in reads:
            _merge(need, b.writes)
        for b in writes:
            _merge(need, b.reads)
            if not b.w_is_dma:
                _merge(need, b.writes)
        self._wait(q, need)
        kind = 'sw' if q == 'pool' else 'hw'
        if kind not in sb.dsems:
            fl = self.free_dsems.setdefault(kind, [])
            sb.dsems[kind] = list(fl.pop()) if fl else [self.newsem("d" + kind), 0]
        ent = sb.dsems[kind]
        ent[1] += 16
        tok = (ent[0], ent[1])
        h = self.semh[tok[0]]
        self.prog[q].append(lambda E, fn=fn, h=h: fn(E).then_inc(h, 16))
        for b in reads:
            if b.reads.get(tok[0], 0) < tok[1]:
                b.reads[tok[0]] = tok[1]
        for b in writes:
            if b.reads or not b.w_is_dma:
                b.writes = {}
            b.writes[tok[0]] = tok[1]
            b.reads = {}
            b.w_is_dma = True

    def barrier(self):
        need = {}
        for e in self.ENG:
            if self.ecount[e] > 0:
                k = self.ecount[e] - 1
                need[self.esems[e][k // SEM_ROT]] = k % SEM_ROT + 1
        for b in self.bufs:
            for ent in b.dsems.values():
                if need.get(ent[0], 0) < ent[1]:
                    need[ent[0]] = ent[1]
            for d in (b.writes, b.reads):
                _merge(need, d)
        for e in self.ENG:
            self._wait(e, dict(need))

    def emit(self):
        nc = self.nc
        with nc.Block() as block:
            @block.tensor
            def _(E):
                for th in self.prog['pe']:
                    th(E)

            @block.scalar
            def _(E):
                for th in self.prog['act']:
                    th(E)

            @block.vector
            def _(E):
                for th in self.prog['dve']:
                    th(E)

            @block.gpsimd
            def _(E):
                for th in self.prog['pool']:
                    th(E)

            @block.sync
            def _(E):
                for th in self.prog['sp']:
                    th(E)


class Arena:
    def __init__(self, ap, sched):
        self.ap = ap
        self.s = sched
        self.off = 0
        self.live = []

    def reset(self):
        self.off = 0
        for b in self.live:
            for kind, ent in b.dsems.items():
                self.s.free_dsems.setdefault(kind, []).append((ent[0], ent[1]))
            b.dsems = {}
            if b in self.s.bufs:
                self.s.bufs.remove(b)
        self.live = []

    def tile(self, name, free_shape, dt):
        n = int(np.prod(free_shape))
        words = n if dt != BF16 else (n + 1) // 2
        assert self.off + words <= ARENA_WORDS, (name, self.off, words)
        a = self.ap[:, self.off:self.off + words]
        self.off += words
        if dt == BF16:
            a = a.bitcast(BF16)
        elif dt == I32:
            a = a.bitcast(I32)
        if len(free_shape) == 2:
            a = a.rearrange("p (a b) -> p a b", a=free_shape[0])
        elif len(free_shape) == 3:
            a = a.rearrange("p (a b c) -> p a b c", a=free_shape[0], b=free_shape[1])
        b = self.s.buf(name)
        self.live.append(b)
        return a, b


class Ctx:
    pass


def build_program():
    nc = bass.Bass("TRN2", target_bir_lowering=False)
    stack = contextlib.ExitStack()
    S = Sched(nc, stack)

    def din(name, shape, dt=F32):
        return nc.dram_tensor(name, list(shape), dt, kind="ExternalInput")

    xseq = din("xseq", [ST, D])
    xwin = din("xwin", [WT, D])
    w_in0 = din("w_in0", [D, 6208])
    wq_r = din("wq_r", [512, 1536])
    wkv_r = din("wkv_r", [512, 2048])
    qng = din("qng", [128, 4])
    kvng = din("kvng", [128, 4])
    tzin = din("tzin", [128, 120 * 64])
    wout0 = din("wout0", [D, D])
    ln0g = din("ln0g", [1, D])
    ln0b = din("ln0b", [1, D])
    w_in1 = din("w_in1", [D, 8192])
    wout1 = din("wout1", [D, D])
    ln1g = din("ln1g", [1, D])
    ln1b = din("ln1b", [1, D])
    cs64s = din("cs64s", [2, 64, ST])
    cs64w = din("cs64w", [2, 64, WT])
    cs128w = din("cs128w", [2, 128, WTX])
    r64 = din("r64", [64, 64])
    r128 = din("r128", [128, 128])
    ident_in = din("ident", [128, 128])
    dilmask = din("dilmask", [20, 128, 512])
    nawin = din("nawin", [128, 64])
    kbias = din("kbias", [128, 40])
    sel_in = din("sel", [128, 4])
    hidx_in = din("hidx", [128, 8], I32)
    smask = din("smask", [32, 128, 512])
    y_own = nc.dram_tensor("y_own", [NOUT, D], F32, kind="ExternalOutput")

    db = {}

    def dscr(name, shape, dt):
        t = nc.dram_tensor(name, list(shape), dt)
        db[name] = S.buf(name)
        return t

    db["y_own"] = S.buf("y_own")
    XTW = dscr("XTW", [128, 16 * WT], BF16)
    XT1 = dscr("XT1", [128, 16 * WT], BF16)
    KA = dscr("KA", [8, 128, WT], BF16)
    QA = dscr("QA", [8, 128, WT], BF16)
    GA = dscr("GA", [8, 128, WT], BF16)
    GB = dscr("GB", [8, 128, WT], BF16)
    QM0 = dscr("QM0", [8, 128, WT], BF16)
    QM1 = dscr("QM1", [8, 64, WT], BF16)
    VA = dscr("VA", [8, WT, 128], BF16)
    KM0 = dscr("KM0", [8, 128, ST], BF16)
    KM1 = dscr("KM1", [64, ST], BF16)
    VM = dscr("VM", [8, ST, 128], BF16)
    YG0 = dscr("YG0", [D, WT], BF16)
    X1 = dscr("X1", [WT, D], F32)
    K1 = dscr("K1", [16, 128, WT], BF16)
    Q1 = dscr("Q1", [16, 128, WTX], BF16)
    G1 = dscr("G1", [16, 128, WTX], BF16)
    V1 = dscr("V1", [16, WT, 128], BF16)
    YG1 = dscr("YG1", [D, WTX], BF16)
    X1H = dscr("X1H", [1024, D], F32)

    arena_t = stack.enter_context(nc.sbuf_tensor("arena", [128, ARENA_WORDS], F32))
    A = Arena(arena_t, S)
    psum = []
    for i in range(8):
        pt = stack.enter_context(nc.psum_tensor(f"ps{i}", [128, 512], F32))
        psum.append((pt, S.buf(f"ps{i}")))
    pctr = [0]

    def nextbank(lo=0, hi=8):
        i = lo + pctr[0] % (hi - lo)
        pctr[0] += 1
        return psum[i]

    rr = [0]

    def rot(engs):
        rr[0] += 1
        return engs[rr[0] % len(engs)]

    def load(q, dst_ap, dst_b, src_ap, src_b=None):
        S.dma(q, lambda E: E.dma_start(out=dst_ap, in_=src_ap), dst_b,
              reads=([src_b] if src_b else []), writes=[dst_b])

    def store(q, dst_ap, dst_b, src_ap, src_b):
        S.dma(q, lambda E: E.dma_start(out=dst_ap, in_=src_ap), src_b, reads=[src_b], writes=[dst_b])

    def load_cast_weight(dst, dst_b, dcol0, src_dram, rows, col0, cols, stg):
        for k in range(rows // 128):
            c0 = 0
            while c0 < cols:
                cw = min(2048, cols - c0)
                wl_ctr[0] += 1
                st, stb = stg[wl_ctr[0] % len(stg)]
                load('sp', st[:, 0:cw], stb, src_dram[k * 128:(k + 1) * 128, col0 + c0:col0 + c0 + cw])
                e = 'act' if (wl_ctr[0] // len(stg)) % 2 == 0 else 'dve'
                o = dst[:, k, dcol0 + c0:dcol0 + c0 + cw]
                if e == 'act':
                    S.op('act', lambda E, o=o, i=st[:, 0:cw]: E.copy(out=o, in_=i), reads=[stb], writes=[dst_b])
                else:
                    S.op(e, lambda E, o=o, i=st[:, 0:cw]: E.tensor_copy(out=o, in_=i), reads=[stb], writes=[dst_b])
                c0 += cw

    wl_ctr = [0]

    def load_weight_coltiles(dst, dst_bufs, dcol0, src_dram, col0, cols, stg):
        g0 = 0
        while g0 < cols:
            gw = min(512, cols - g0)
            for k0 in range(0, 16, 4):
                wl_ctr[0] += 1
                st, stb = stg[wl_ctr[0] % len(stg)]
                stv = st[:, 0:4 * gw].rearrange("p (k c) -> p k c", k=4)
                load('sp', stv, stb,
                     src_dram[k0 * 128:(k0 + 4) * 128, col0 + g0:col0 + g0 + gw].rearrange("(k p) c -> p k c", p=128))
                o = dst[:, k0:k0 + 4, dcol0 + g0:dcol0 + g0 + gw]
                buf = dst_bufs[(dcol0 + g0) // 512]
                if (wl_ctr[0] // len(stg)) % 2 == 0:
                    S.op('act', lambda E, o=o, i=stv: E.copy(out=o, in_=i), reads=[stb], writes=[buf])
                else:
                    S.op('dve', lambda E, o=o, i=stv: E.tensor_copy(out=o, in_=i), reads=[stb], writes=[buf])
            g0 += gw

    def gemm_pass(tiles, src, src_b, wblocks, body, save_xT=None, load_xT=None, use_tm=0, use_lat=None,
                  rope_dim=0, cs_t=None, post=None, row_tiles=None):
        S.barrier()
        A.reset()
        c = Ctx()
        bank_hi = 7 if use_lat is not None else 8
        ncols = sum(w[2] for w in wblocks)
        c.Wb, c.Wb_b = A.tile("Wb", [16, ncols], BF16)
        stg = [A.tile(f"stg{i}", [2048], F32) for i in range(2)]
        inits = []
        row_tiles = row_tiles or {}
        if load_xT is None or row_tiles:
            identb, identb_b = A.tile("identb", [128], BF16)
            xb = [A.tile(f"xb{i}", [2048], BF16) for i in range(4)]
            def init_ident():
                load('sp', stg[0][0][:, 0:128], stg[0][1], ident_in[:, :])
                S.op('dve', lambda E: E.tensor_copy(out=identb, in_=stg[0][0][:, 0:128]), reads=[stg[0][1]], writes=[identb_b])
            inits.append(init_ident)
        xT = [A.tile(f"xT{i}", [16, 512], BF16) for i in range(2)]
        outs = [A.tile(f"o{i}", [512], BF16) for i in range(8)]
        c.Wbs = [S.buf(f"Wbc{j}") for j in range((ncols + 511) // 512)]
        A.live.extend(c.Wbs)

        def init_weights():
            dc = 0
            for (wt_, c0_, n_) in wblocks:
                load_weight_coltiles(c.Wb, c.Wbs, dc, wt_, c0_, n_, stg)
                dc += n_
        inits.append(init_weights)
        if rope_dim:
            cosT = [A.tile(f"cos{i}", [512], F32) for i in range(2)]
            sinT = [A.tile(f"sin{i}", [512], F32) for i in range(2)]
            ra = [A.tile(f"ra{i}", [512], F32) for i in range(2)]
            rt1 = [A.tile(f"rt1{i}", [512], F32) for i in range(2)]
            rt2 = [A.tile(f"rt2{i}", [512], F32) for i in range(2)]
            rmat, rmat_b = A.tile("rmat", [128], F32)
            inits.append(lambda: load('sp', rmat[0:rope_dim, 0:rope_dim], rmat_b, (r64 if rope_dim == 64 else r128)[:, :]))
        if use_tm:
            vouts = [A.tile(f"vo{i}", [4, use_tm], BF16) for i in range(2)]
        if use_lat is not None:
            wup_d, nup, g_d = use_lat
            c.wup, c.wup_b = A.tile("wup", [4, nup], BF16)
            gg, gg_b = A.tile("gg", [4], F32)
            onesf, onesf_b = A.tile("onesf", [128], F32)

            def init_lat():
                load_cast_weight(c.wup, c.wup_b, 0, wup_d, 512, 0, nup, stg)
                load('sp', gg, gg_b, g_d[:, :])
                S.op('pool', lambda E: E.memset(onesf, 1.0), writes=[onesf_b])
            inits.append(init_lat)
            latT = [A.tile(f"lat{i}", [4, 512], F32) for i in range(2)]
            sqT = [A.tile(f"sq{i}", [4, 512], F32) for i in range(2)]
            rstdT = [A.tile(f"rstd{i}", [512], F32) for i in range(2)]
            latnT = [A.tile(f"latn{i}", [4, 512], BF16) for i in range(2)]
            c.latnT = latnT
            c.lat_tail = {}
        octr = [0]

        def nxt_out():
            octr[0] += 1
            return outs[octr[0] % len(outs)]

        def fm(xTt, xTb, c0, M):
            pt, pb = nextbank(0, bank_hi)
            for k in range(16):
                S.op('pe', lambda E, o=pt[0:M, :], l=c.Wb[:, k, c0:c0 + M], r=xTt[:, k, :], st=(k == 0), sp=(k == 15):
                     E.matmul(o, l, r, start=st, stop=sp), reads=[c.Wbs[c0 // 512], xTb], writes=[pb])
            return pt, pb

        def fm_up(c0, M, slot):
            pt, pb = nextbank(0, bank_hi)
            latn, latn_b = c.latnT[slot]
            for k in range(4):
                S.op('pe', lambda E, o=pt[0:M, :], l=c.wup[:, k, c0:c0 + M], r=latn[:, k, :], st=(k == 0), sp=(k == 3):
                     E.matmul(o, l, r, start=st, stop=sp), reads=[c.wup_b, latn_b], writes=[pb])
            return pt, pb

        def evac_store(pt, pb, M, dst_ap, dst_b, func=None, scale=1.0):
            o, ob = nxt_out()
            fn = func if func is not None else AF.Copy
            S.op('act', lambda E, o=o[0:M, :], i=pt[0:M, :]: E.activation(out=o, in_=i, func=fn, scale=scale),
                 reads=[pb], writes=[ob])
            store('pool', dst_ap, dst_b, o[0:M, :], ob)

        c.rope_pending = []

        def rope_store(pt, pb, M, dst_ap, dst_b, scale, slot):
            a, ab = ra[slot]
            t1, t1b = rt1[slot]
            t2, t2b = rt2[slot]
            cb, sb_ = c.cb, c.sb
            S.op('act', lambda E: E.activation(out=a[0:M, :], in_=pt[0:M, :], func=AF.Copy, scale=scale),
                 reads=[pb], writes=[ab])

            def tail():
                p2, p2b = nextbank(0, bank_hi)
                S.op('pe', lambda E: E.matmul(p2[0:M, :], rmat[0:M, 0:M], a[0:M, :], start=True, stop=True),
                     reads=[rmat_b, ab], writes=[p2b])
                S.op('pool', lambda E: E.tensor_tensor(out=t1[0:M, :], in0=a[0:M, :], in1=cb[0][0:M, :], op=ALU.mult),
                     reads=[ab, cb[1]], writes=[t1b])
                S.op('dve', lambda E: E.tensor_tensor(out=t2[0:M, :], in0=p2[0:M, :], in1=sb_[0][0:M, :], op=ALU.mult),
                     reads=[p2b, sb_[1]], writes=[t2b])
                o, ob = nxt_out()
                S.op('dve', lambda E: E.tensor_tensor(out=o[0:M, :], in0=t1[0:M, :], in1=t2[0:M, :], op=ALU.add),
                     reads=[t1b, t2b], writes=[ob])
                store('pool', dst_ap, dst_b, o[0:M, :], ob)
            c.rope_pending.append(tail)
            while len(c.rope_pending) > 1:
                c.rope_pending.pop(0)()

        def rope_flush():
            while c.rope_pending:
                c.rope_pending.pop(0)()

        def tm(lhs, lhsb, nk, W, Wb_, c0, ncols, dsts, vslot):
            vo, vob = vouts[vslot % 2]
            for sub in range(4):
                for g0 in range(0, ncols, 512):
                    gw = min(512, ncols - g0)
                    pt, pb = nextbank(0, bank_hi)
                    for k in range(nk):
                        S.op('pe', lambda E, o=pt[:, 0:gw], l=lhs[:, k, sub * 128:(sub + 1) * 128],
                             r=W[:, k, c0 + g0:c0 + g0 + gw], st=(k == 0), sp=(k == nk - 1):
                             E.matmul(o, l, r, start=st, stop=sp),
                             reads=[lhsb] + ([Wb_[(c0 + g0) // 512]] if isinstance(Wb_, list) else [Wb_]),
                             writes=[pb])
                    S.op('dve', lambda E, o=vo[:, sub, g0:g0 + gw], i=pt[:, 0:gw]: E.tensor_copy(out=o, in_=i),
                         reads=[pb], writes=[vob])
            for j, (dap, dbuf) in enumerate(dsts):
                store('pool', dap.rearrange("(t p) d -> p t d", p=128), dbuf, vo[:, :, j * 128:(j + 1) * 128], vob)

        def lat_gemm(xTt, xTb, cbase, slot):
            pss, pssb = psum[7]
            lat, lat_b = latT[slot]
            sq, sq_b = sqT[slot]
            rstd, rstd_b = rstdT[slot]
            latn, latn_b = latnT[slot]

            def ssq(cc):
                S.op('pe', lambda E, r=sq[:, cc, :], st=(cc == 0), sp=(cc == 3), p_=pss[:, :]:
                     E.matmul(p_, onesf, r, start=st, stop=sp), reads=[onesf_b, sq_b], writes=[pssb])
            for cc in range(4):
                pt, pb = fm(xTt, xTb, cbase + cc * 128, 128)
                S.op('act', lambda E, o=lat[:, cc, :], i=pt[:, :]: E.copy(out=o, in_=i), reads=[pb], writes=[lat_b])
                S.op('dve', lambda E, o=sq[:, cc, :], i=lat[:, cc, :]: E.tensor_tensor(out=o, in0=i, in1=i, op=ALU.mult),
                     reads=[lat_b], writes=[sq_b])
                if cc > 0:
                    ssq(cc - 1)

            def tail():
                ssq(3)
                S.op('dve', lambda E, p_=pss[:, :]: E.tensor_scalar(out=rstd, in0=p_, scalar1=1.0 / 512, scalar2=RMS_EPS,
                                                                    op0=ALU.mult, op1=ALU.add), reads=[pssb], writes=[rstd_b])
                S.op('act', lambda E: E.activation(out=rstd, in_=rstd, func=AF.Sqrt), reads=[rstd_b], writes=[rstd_b])
                S.op('dve', lambda E: E.reciprocal(out=rstd, in_=rstd), reads=[rstd_b], writes=[rstd_b])
                for cc in range(4):
                    S.op('dve', lambda E, o=latn[:, cc, :], i=lat[:, cc, :], g=gg[:, cc:cc + 1]:
                         E.scalar_tensor_tensor(out=o, in0=i, scalar=g, in1=rstd, op0=ALU.mult, op1=ALU.mult),
                         reads=[lat_b, gg_b, rstd_b], writes=[latn_b])
            c.lat_tail[slot] = tail

        def lat_finish(slot):
            c.lat_tail.pop(slot)()

        while len(stg) < 4 and ARENA_WORDS - A.off >= 2048 + 64:
            stg.append(A.tile(f"stg{len(stg)}", [2048], F32))
        for f_ in inits:
            f_()
        c.fm, c.fm_up, c.evac_store, c.rope_store, c.tm = fm, fm_up, evac_store, rope_store, tm
        c.lat_gemm, c.lat_finish, c.rope_flush = (lat_gemm, lat_finish, rope_flush) if use_lat is not None else (None, None, rope_flush)
        for ti, t in enumerate(tiles):
            xTt, xTb = xT[ti % 2]
            tok0 = t * 512
            if rope_dim:
                c.cb, c.sb = cosT[ti % 2], sinT[ti % 2]
                load('sp', c.cb[0][0:rope_dim, :], c.cb[1], cs_t[0, :, tok0:tok0 + 512])
                load('sp', c.sb[0][0:rope_dim, :], c.sb[1], cs_t[1, :, tok0:tok0 + 512])
            if load_xT is not None and t not in row_tiles:
                load('sp', xTt, xTb, load_xT.ap().rearrange("p (k t) -> p k t", k=16)[:, :, tok0:tok0 + 512],
                     db[load_xT.name])
            else:
                if t in row_tiles:
                    rsrc_, rrow0, rbuf_ = row_tiles[t]
                else:
                    rsrc_, rrow0, rbuf_ = src, tok0, src_b
                for sub in range(4):
                    xfi, xfb = stg[sub % len(stg)]
                    xbi, xbb = xb[sub]
                    load('sp', xfi, xfb, rsrc_[rrow0 + sub * 128:rrow0 + (sub + 1) * 128, :], rbuf_)
                    S.op('act', lambda E, o=xbi, i=xfi: E.copy(out=o, in_=i), reads=[xfb], writes=[xbb])
                    for half in range(2):
                        pt, pb = nextbank(0, bank_hi)
                        ptb = pt[:, :].bitcast(BF16)
                        for j in range(8):
                            k = half * 8 + j
                            S.op('pe', lambda E, o=ptb[:, j * 128:(j + 1) * 128], i=xbi[:, k * 128:(k + 1) * 128]:
                                 E.transpose(o, i, identb), reads=[xbb, identb_b], writes=[pb])
                        S.op('dve', lambda E, o=xTt[:, half * 8:(half + 1) * 8, sub * 128:(sub + 1) * 128],
                             i=ptb.rearrange("p (a b) -> p a b", a=8): E.tensor_copy(out=o, in_=i),
                             reads=[pb], writes=[xTb])
                if save_xT is not None and t not in row_tiles:
                    store('pool', save_xT.ap().rearrange("p (k t) -> p k t", k=16)[:, :, tok0:tok0 + 512],
                          db[save_xT.name], xTt, xTb)
            body(c, ti, t, xTt, xTb)
        if post is not None:
            post(c)
        while c.rope_pending:
            c.rope_pending.pop(0)()

    def attn_phase(units, alloc_extra=None):
        S.barrier()
        A.reset()
        onesb, onesb_b = A.tile("onesb", [128], BF16)
        S.op('pool', lambda E: E.memset(onesb, 1.0), writes=[onesb_b])
        masks = alloc_extra(A) if alloc_extra else {}
        nK = max(len(u["K"]) for u in units)
        Kt = [[A.tile(f"K{i}_{b}", [8192], BF16) for i in range(nK)] for b in range(2)]
        Vt = [A.tile(f"V{b}", [64, 128], BF16) for b in range(2)]
        Qt = [[A.tile(f"Q{i}_{b}", [512], BF16) for i in range(nK)] for b in range(2)]
        Gt = [A.tile(f"G{b}", [512], BF16) for b in range(2)]
        Et = [A.tile(f"E{b}", [512], BF16) for b in range(6)]
        Pt = [A.tile(f"P{b}", [512], BF16) for b in range(6)]
        rec = [A.tile(f"rec{b}", [512], F32) for b in range(2)]
        yv = [A.tile(f"yv{b}", [512], F32) for b in range(2)]
        yo = [A.tile(f"yo{b}", [512], BF16) for b in range(2)]
        if any(u.get("dacc") for u in units):
            dacc = [A.tile(f"dacc{b}", [512], F32) for b in range(2)]
            onesf32, onesf32_b = A.tile("onesf32", [128], F32)
            S.op('pool', lambda E: E.memset(onesf32, 1.0), writes=[onesf32_b])
        qctr = 0
        ectr = 0
        LAG = 4
        pending = []

        def push(fn):
            pending.append(fn)
            while len(pending) > LAG:
                pending.pop(0)()

        def load_kv(ui):
            u = units[ui]
            kb = ui % 2
            nkt = u["nkt"]
            for i, (kap, kbuf, parts) in enumerate(u["K"]):
                load('sp', Kt[kb][i][0][0:parts, 0:nkt * 128], Kt[kb][i][1], kap, kbuf)
            vt, vtb = Vt[kb]
            load('sp', vt[:, 0:nkt, :], vtb, u["V"][0].rearrange("(t p) d -> p t d", p=128), u["V"][1])

        load_kv(0)
        step_units = [i for i, u_ in enumerate(units) if u_.get("prep_steps")]
        plan = {}
        prev = 0
        for j in step_units:
            steps = units[j]["prep_steps"](masks)
            if j == 0:
                na_ctx["force_dve"] = True
                for st_ in steps:
                    st_()
                na_ctx["force_dve"] = False
            else:
                slots = [(i, qi_) for i in range(prev, j) for qi_ in range(len(units[i]["qtiles"]))]
                per = -(-len(steps) // max(1, len(slots)))
                for si, key in enumerate(slots):
                    plan[key] = steps[si * per:(si + 1) * per]
                rest = steps[len(slots) * per:]
                if rest:
                    plan[slots[-1]] = plan[slots[-1]] + rest
            prev = j
        for ui, u in enumerate(units):
            if u.get("prep"):
                while pending:
                    pending.pop(0)()
                u["prep"](masks)
            kb = ui % 2
            vt, vtb = Vt[kb]
            nparts = len(u["K"])
            for qi, q in enumerate(u["qtiles"]):
                if qi == 1 and ui + 1 < len(units):
                    load_kv(ui + 1)
                qb = qctr % 2
                qctr += 1
                for i, (qap, qbuf, parts) in enumerate(q["Q"]):
                    load('sp', Qt[qb][i][0][0:parts, :], Qt[qb][i][1], qap, qbuf)
                gt, gtb = Gt[qb]
                load('sp', gt, gtb, q["G"][0], q["G"][1])
                po, pob = psum[4 + 2 * qb]
                pd, pdb = psum[5 + 2 * qb]
                kl = q["keys"]
                for ki, (kt, mk, bias) in enumerate(kl):
                    ps, psb = nextbank(0, 4)
                    for i, (kap, kbuf, parts) in enumerate(u["K"]):
                        S.op('pe', lambda E, o=ps[:, :], l=Kt[kb][i][0][0:parts, kt * 128:(kt + 1) * 128],
                             r=Qt[qb][i][0][0:parts, :], st=(i == 0), sp=(i == nparts - 1):
                             E.matmul(o, l, r, start=st, stop=sp),
                             reads=[Kt[kb][i][1], Qt[qb][i][1]], writes=[psb])
                    eb = ectr % 6
                    ectr += 1
                    pm, pmb = Pt[eb]
                    tgt, tgtb = (pm, pmb) if mk is None else Et[eb]
                    rds = [psb]
                    if bias is not None:
                        rds.append(bias[1])
                        S.op('act', lambda E, o=tgt, i=ps[:, :], b=bias[0]: E.activation(out=o, in_=i, func=AF.Exp, bias=b),
                             reads=rds, writes=[tgtb])
                    else:
                        S.op('act', lambda E, o=tgt, i=ps[:, :]: E.activation(out=o, in_=i, func=AF.Exp),
                             reads=rds, writes=[tgtb])
                    if mk is not None:
                        mt, mtb = masks[mk]
                        S.op('dve', lambda E, o=pm, a=tgt, m=mt: E.tensor_tensor(out=o, in0=a, in1=m, op=ALU.mult),
                             reads=[tgtb, mtb], writes=[pmb])

                    if u.get("dacc"):
                        ac, acb = dacc[qb]
                        if ki == 0:
                            S.op('dve', lambda E, ac=ac, pm=pm: E.tensor_copy(out=ac, in_=pm), reads=[pmb], writes=[acb])
                        else:
                            S.op('dve', lambda E, ac=ac, pm=pm: E.tensor_tensor(out=ac, in0=ac, in1=pm, op=ALU.add),
                                 reads=[pmb, acb], writes=[acb])

                    def pv(po=po, pob=pob, pd=pd, pdb=pdb, vt=vt, vtb=vtb, kt=kt, pm=pm, pmb=pmb,
                           st=(ki == 0), sp=(ki == len(kl) - 1), use_acc=bool(u.get("dacc")), qb=qb):
                        S.op('pe', lambda E: E.matmul(po[:, :], vt[:, kt, :], pm, start=st, stop=sp),
                             reads=[vtb, pmb], writes=[pob])
                        if not use_acc:
                            S.op('pe', lambda E: E.matmul(pd[:, :], onesb, pm, start=st, stop=sp),
                                 reads=[onesb_b, pmb], writes=[pdb])
                        elif sp:
                            ac, acb = dacc[qb]
                            S.op('pe', lambda E: E.matmul(pd[:, :], onesf32, ac, start=True, stop=True),
                                 reads=[onesf32_b, acb], writes=[pdb])
                    push(pv)

                def fin(qb=qb, po=po, pob=pob, pd=pd, pdb=pdb, gt=gt, gtb=gtb, q=q):
                    rc, rcb = rec[qb]
                    y, yb = yv[qb]
                    o, ob = yo[qb]
                    S.op('dve', lambda E: E.tensor_scalar_max(out=rc, in0=pd[:, :], scalar1=1e-30),
                         reads=[pdb], writes=[rcb])
                    S.op('act', lambda E: E.activation(out=rc, in_=rc, func=AF.Ln), reads=[rcb], writes=[rcb])
                    S.op('act', lambda E: E.activation(out=rc, in_=rc, func=AF.Exp, scale=-1.0), reads=[rcb], writes=[rcb])
                    S.op('dve', lambda E: E.tensor_tensor(out=y, in0=po[:, :], in1=rc, op=ALU.mult),
                         reads=[pob, rcb], writes=[yb])
                    S.op('pool', lambda E: E.tensor_tensor(out=o, in0=y, in1=gt, op=ALU.mult),
                         reads=[yb, gtb], writes=[ob])
                    store('pool', q["out"][0], q["out"][1], o, ob)
                push(fin)
                for st_ in plan.get((ui, qi), []):
                    st_()
        while pending:
            pending.pop(0)()

    def outproj_phase(YG, Wd, lg, lb_, tiles, resid, outdst):
        S.barrier()
        A.reset()
        Wb, Wb_b = A.tile("Wo", [16, 2048], BF16)
        stg = [A.tile(f"stg{i}", [2048], F32) for i in range(2)]
        Wbs = [S.buf(f"Woc{j}") for j in range(4)]
        A.live.extend(Wbs)
        load_weight_coltiles(Wb, Wbs, 0, Wd, 0, D, stg)
        gbc, gbc_b = A.tile("gbc", [2048], F32)
        bbc, bbc_b = A.tile("bbc", [2048], F32)
        load('sp', gbc, gbc_b, lg.ap()[0:1, :].partition_broadcast(128)[:, 0, :])
        load('sp', bbc, bbc_b, lb_.ap()[0:1, :].partition_broadcast(128)[:, 0, :])
        ygT = [A.tile(f"ygT{i}", [16, 512], BF16) for i in range(2)]
        tt_ = [A.tile(f"t{i}", [2048], F32) for i in range(2)]
        oo = [A.tile(f"oo{i}", [2048], F32) for i in range(2)]
        stats = [A.tile(f"st{i}", [4, 6], F32) for i in range(2)]
        mv = [A.tile(f"mv{i}", [2], F32) for i in range(2)]
        rs = [A.tile(f"rs{i}", [1], F32) for i in range(2)]
        nm_ = [A.tile(f"nm{i}", [1], F32) for i in range(2)]
        sctr = 0
        tails = []
        for ti, t in enumerate(tiles):
            yt, ytb = ygT[ti % 2]
            load('sp', yt, ytb, YG.ap().rearrange("(k p) t -> p k t", p=128)[:, :, t * 512:(t + 1) * 512], db[YG.name])
            rsrc, rsb_ = resid(t)
            odst, odb = outdst(ti, t)
            for sub in range(4):
                sl_ = sctr % 2
                sctr += 1
                xr, xrb = stg[sl_]
                load('sp', xr, xrb, rsrc[sub * 128:(sub + 1) * 128, :], rsb_)
                t_, tb = tt_[sl_]
                for nb in range(4):
                    pt, pb = psum[nb + 4 * (sctr % 2)]
                    for ch in range(16):
                        S.op('pe', lambda E, o=pt[:, :], l=yt[:, ch, sub * 128:(sub + 1) * 128],
                             rr_=Wb[:, ch, nb * 512:(nb + 1) * 512], st=(ch == 0), sp=(ch == 15):
                             E.matmul(o, l, rr_, start=st, stop=sp), reads=[ytb, Wbs[nb]], writes=[pb])
                    S.op('dve', lambda E, o=t_[:, nb * 512:(nb + 1) * 512], a=xr[:, nb * 512:(nb + 1) * 512], b=pt[:, :]:
                         E.scalar_tensor_tensor(out=o, in0=a, scalar=ALPHA, in1=b, op0=ALU.mult, op1=ALU.add),
                         reads=[xrb, pb], writes=[tb])
                st, stb = stats[sl_]
                m, mb = mv[sl_]
                rsd, rsb = rs[sl_]
                nmm, nmb = nm_[sl_]
                for nb in range(4):
                    S.op('dve', lambda E, o=st[:, nb, :], a=t_[:, nb * 512:(nb + 1) * 512]: E.bn_stats(out=o, in_=a),
                         reads=[tb], writes=[stb])
                S.op('dve', lambda E, o=m, a=st: E.bn_aggr(out=o, in_=a), reads=[stb], writes=[mb])
                S.op('dve', lambda E, o=rsd, a=m[:, 1:2]: E.tensor_scalar_add(out=o, in0=a, scalar1=LN_EPS),
                     reads=[mb], writes=[rsb])
                S.op('act', lambda E, o=rsd: E.activation(out=o, in_=o, func=AF.Sqrt), reads=[rsb], writes=[rsb])
                while tails:
                    tails.pop(0)()
                S.op('dve', lambda E, o=rsd: E.reciprocal(out=o, in_=o), reads=[rsb], writes=[rsb])
                S.op('dve', lambda E, o=nmm, a=m[:, 0:1], b=rsd: E.tensor_scalar(out=o, in0=a, scalar1=-1.0, scalar2=b,
                                                                               op0=ALU.mult, op1=ALU.mult),
                     reads=[mb, rsb], writes=[nmb])
                o, ob = oo[sl_]
                S.op('act', lambda E, o=o, a=t_, sc=rsd, bi=nmm: E.activation(out=o, in_=a, func=AF.Identity, bias=bi, scale=sc),
                     reads=[tb, rsb, nmb], writes=[ob])

                def tail(o=o, ob=ob, dst=odst[sub * 128:(sub + 1) * 128, :], odb=odb):
                    S.op('dve', lambda E: E.tensor_tensor(out=o, in0=o, in1=gbc, op=ALU.mult),
                         reads=[ob, gbc_b], writes=[ob])
                    S.op('dve', lambda E: E.tensor_tensor(out=o, in0=o, in1=bbc, op=ALU.add),
                         reads=[ob, bbc_b], writes=[ob])
                    store('pool', dst, odb, o, ob)
                tails.append(tail)
        while tails:
            tails.pop(0)()

    def tcols(t):
        return slice(t * 512, (t + 1) * 512)

    A_tiles = list(range(ST // 512))

    def up_A(c, ti):
        t = A_tiles[ti]
        for h in range(8):
            pt, pb = c.fm_up(h * 128, 128, ti % 2)
            c.evac_store(pt, pb, 128, KM0[h, :, tcols(t)], db["KM0"])
        c.tm(c.latnT[ti % 2][0], c.latnT[ti % 2][1], 4, c.wup, c.wup_b, 1024, 1024,
             [(VM[h, t * 512:(t + 1) * 512, :], db["VM"]) for h in range(8)], ti)

    def body_A(c, ti, t, xTt, xTb):
        c.lat_gemm(xTt, xTb, 0, ti % 2)
        pt, pb = c.fm(xTt, xTb, 512, 64)
        c.rope_store(pt, pb, 64, KM1[:, tcols(t)], db["KM1"], 1.0, ti % 2)
        if ti > 0:
            up_A(c, ti - 1)
        c.lat_finish(ti % 2)

    gemm_pass(A_tiles, xseq, None, [(w_in0, 4608, 576)], body_A, use_tm=1024,
              use_lat=(wkv_r, 2048, kvng), rope_dim=64, cs_t=cs64s, post=lambda c: up_A(c, len(A_tiles) - 1))

    def body_B1(c, ti, t, xTt, xTb):
        for h in range(8):
            pt, pb = c.fm(xTt, xTb, h * 128, 128)
            c.evac_store(pt, pb, 128, KA[h, :, tcols(t)], db["KA"])
        c.tm(xTt, xTb, 16, c.Wb, c.Wbs, 1024, 1024,
             [(VA[h, t * 512:(t + 1) * 512, :], db["VA"]) for h in range(8)], ti)

    gemm_pass(list(range(WT // 512)), xwin, None, [(w_in0, 1024, 2048)], body_B1, save_xT=XTW, use_tm=1024)

    def body_B2(c, ti, t, xTt, xTb):
        for h in range(8):
            pt, pb = c.fm(xTt, xTb, h * 128, 128)
            c.evac_store(pt, pb, 128, QA[h, :, tcols(t)], db["QA"], scale=128.0 ** -0.5)
        for h in range(8):
            pt, pb = c.fm(xTt, xTb, 1024 + h * 128, 128)
            c.evac_store(pt, pb, 128, GA[h, :, tcols(t)], db["GA"], func=AF.Silu)

    gemm_pass(W0_TILES, None, None, [(w_in0, 0, 1024), (w_in0, 3072, 1024)], body_B2, load_xT=XTW)

    def up_B3(c, ti):
        t = W0_TILES[ti]
        sc = 192.0 ** -0.5
        for h in range(8):
            pt, pb = c.fm_up(h * 128, 128, ti % 2)
            c.evac_store(pt, pb, 128, QM0[h, :, tcols(t)], db["QM0"], scale=sc)
        for h in range(8):
            pt, pb = c.fm_up(1024 + h * 64, 64, ti % 2)
            c.rope_store(pt, pb, 64, QM1[h, :, tcols(t)], db["QM1"], sc, h % 2)

    def body_B3(c, ti, t, xTt, xTb):
        c.lat_gemm(xTt, xTb, 0, ti % 2)
        for h in range(4):
            pt, pb = c.fm(xTt, xTb, 512 + h * 128, 128)
            c.evac_store(pt, pb, 128, GB[h, :, tcols(t)], db["GB"], func=AF.Silu)
        c.lat_finish(ti % 2)
        for h in range(4, 8):
            pt, pb = c.fm(xTt, xTb, 512 + h * 128, 128)
            c.evac_store(pt, pb, 128, GB[h, :, tcols(t)], db["GB"], func=AF.Silu)
        up_B3(c, ti)

    gemm_pass(W0_TILES, None, None, [(w_in0, 4096, 512), (w_in0, 5184, 1024)], body_B3, load_xT=XTW,
              use_lat=(wq_r, 1536, qng), rope_dim=64, cs_t=cs64w)

    def na_blocks(kind, R=32):
        res = []
        for ql in range(8):
            if kind == "top":
                r, base = ql, 0
                r0 = min(max(r - 4, 0), R - 8)
            elif kind == "int":
                r, base = 8 + ql, 4
                r0 = r - 4
            else:
                r, base = R - 8 + ql, R - 12
                r0 = min(max(r - 4, 0), R - 8)
            for kr in range(r0, r0 + 8):
                res.append(((kr - base) // 2, (kr - base) % 2, ql, kr - r + 7))
        return res

    na_ctx = {}

    def na_alloc(A_):
        gall, gall_b = A_.tile("gall", [120, 64], F32)
        win, winb = A_.tile("win", [64], F32)
        selt, selb = A_.tile("selt", [4], F32)
        load('sp', win, winb, nawin[:, :])
        load('sp', selt, selb, sel_in[:, :])
        load('sp', gall, gall_b, tzin.ap().rearrange("p (a b) -> p a b", a=120))
        S.op('act', lambda E: E.activation(out=gall, in_=gall, func=AF.Exp), reads=[gall_b], writes=[gall_b])
        for i in range(120):
            S.op('dve', lambda E, o=gall[:, i, :]: E.tensor_tensor(out=o, in0=o, in1=win, op=ALU.mult),
                 reads=[gall_b, winb], writes=[gall_b])
        masks = {}
        for par in range(2):
            for kind, nt_ in (("top", 6), ("int", 8), ("bot", 6), ("m3", 8), ("m6", 8)):
                for t in range(nt_):
                    m, mb = A_.tile(f"m{par}{kind}{t}", [512], BF16)
                    masks[(par, kind, t)] = (m, mb)
                    if kind in ("top", "int", "bot"):
                        S.op('dve', lambda E, m=m: E.memset(m, 0.0), writes=[mb])
        na_ctx.update(gall=gall, gall_b=gall_b, selt=selt, selb=selb)
        return masks

    def na_prep_steps(h):
        def make(masks):
            gall, gall_b = na_ctx["gall"], na_ctx["gall_b"]
            selt, selb = na_ctx["selt"], na_ctx["selb"]
            par = h % 2
            steps = []
            group = []
            for kind in ("top", "int", "bot"):
                for (t, a, ql, dr) in na_blocks(kind):
                    m, mb = masks[(par, kind, t)]

                    def cp(m=m, mb=mb, a=a, ql=ql, dr=dr):
                        S.op('dve' if na_ctx.get("force_dve") else rot(['dve', 'pool']),
                             lambda E, o=m[a * 64:(a + 1) * 64, ql * 64:(ql + 1) * 64], i=gall[a * 64:(a + 1) * 64, h * 15 + dr, :]:
                             E.tensor_copy(out=o, in_=i), reads=[gall_b], writes=[mb])
                    group.append(cp)
                    if len(group) == 8:
                        steps.append(lambda g=tuple(group): [f_() for f_ in g])
                        group = []
            if group:
                steps.append(lambda g=tuple(group): [f_() for f_ in g])
            for dst, src, toff, nsrc, s_on, s_off in (("m3", "top", 2, 6, 0, 1), ("m6", "bot", 0, 6, 2, 3)):
                for t in range(8):
                    def bl(dst=dst, src=src, toff=toff, nsrc=nsrc, s_on=s_on, s_off=s_off, t=t):
                        m, mb = masks[(par, dst, t)]
                        mi, mib = masks[(par, "int", t)]
                        S.op('dve', lambda E, sa=selt[:, s_off:s_off + 1]: E.tensor_scalar(out=m, in0=mi, scalar1=sa, scalar2=None,
                                                                                        op0=ALU.mult), reads=[mib, selb], writes=[mb])
                        ts_ = t - toff
                        if 0 <= ts_ < nsrc:
                            ms, msb = masks[(par, src, ts_)]
                            S.op('dve', lambda E, sb2=selt[:, s_on:s_on + 1]:
                                 E.scalar_tensor_tensor(out=m, in0=ms, scalar=sb2, in1=m, op0=ALU.mult, op1=ALU.add),
                                 reads=[mb, msb, selb], writes=[mb])
                    steps.append(bl)
            return steps
        return make

    na_units = []
    for h in range(8):
        par = h % 2
        qts = []
        for qt in P_W0:
            kind = "m3" if qt == 3 else ("m6" if qt == 6 else "int")
            qts.append(dict(Q=[(QA[h, :, tcols(qt)], db["QA"], 128)], G=(GA[h, :, tcols(qt)], db["GA"]),
                            keys=[(4 * qt - 2 + t, (par, kind, t), None) for t in range(8)],
                            out=(YG0[h * 128:(h + 1) * 128, tcols(qt)], db["YG0"])))
        na_units.append(dict(K=[(KA[h, :, 0:5120], db["KA"], 128)], nkt=40, V=(VA[h, 0:5120, :], db["VA"]),
                             prep_steps=na_prep_steps(h), qtiles=qts))
        qts = []
        for ql in range(4):
            wt = 10 + ql
            if ql == 0:
                keys = [(t, (par, "top", t), None) for t in range(6)]
            elif ql == 3:
                keys = [(10 + t, (par, "bot", t), None) for t in range(6)]
            else:
                keys = [(4 * ql - 2 + t, (par, "int", t), None) for t in range(8)]
            qts.append(dict(Q=[(QA[h, :, tcols(wt)], db["QA"], 128)], G=(GA[h, :, tcols(wt)], db["GA"]), keys=keys,
                            out=(YG0[h * 128:(h + 1) * 128, tcols(wt)], db["YG0"])))
        na_units.append(dict(K=[(KA[h, :, 5120:7168], db["KA"], 128)], nkt=16, V=(VA[h, 5120:7168, :], db["VA"]),
                             prep=None, qtiles=qts))
    attn_phase(na_units, na_alloc)

    mla_units = []
    for h in range(8):
        for (kq0, nkt, wts) in ((0, 64, P_W0), (8192, 16, S_WT)):
            qts = []
            for wt in wts:
                qts.append(dict(Q=[(QM0[h, :, tcols(wt)], db["QM0"], 128), (QM1[h, :, tcols(wt)], db["QM1"], 64)],
                                G=(GB[h, :, tcols(wt)], db["GB"]), keys=[(t, None, None) for t in range(nkt)],
                                out=(YG0[1024 + h * 128:1024 + (h + 1) * 128, tcols(wt)], db["YG0"])))
            mla_units.append(dict(K=[(KM0[h, :, kq0:kq0 + nkt * 128], db["KM0"], 128),
                                     (KM1[:, kq0:kq0 + nkt * 128], db["KM1"], 64)], nkt=nkt,
                                  V=(VM[h, kq0:kq0 + nkt * 128, :], db["VM"]), prep=None, qtiles=qts))
    attn_phase(mla_units, None)

    outproj_phase(YG0, wout0, ln0g, ln0b, W0_TILES,
                  lambda t: (xwin[t * 512:(t + 1) * 512, :], None),
                  lambda ti, t: (X1[t * 512:(t + 1) * 512, :], db["X1"]))

    S.barrier()
    A.reset()
    hi_t, hi_b = A.tile("hidx", [8], I32)
    load('sp', hi_t, hi_b, hidx_in[:, :])
    gwide, gwide_b = A.tile("gwide", [8, 2048], F32)

    def gfn(E):
        return E.indirect_dma_start(out=gwide.rearrange("p j d -> p (j d)"), out_offset=None,
                                    in_=X1.ap().rearrange("(r j) d -> r (j d)", j=8),
                                    in_offset=bass.IndirectOffsetOnAxis(ap=hi_t[:, 0:1], axis=0))
    S.dma('pool', gfn, gwide_b, reads=[db["X1"], hi_b], writes=[gwide_b])
    store('sp', X1H.ap().rearrange("(p j) d -> p j d", j=8), db["X1H"], gwide, gwide_b)

    def body_C1(c, ti, t, xTt, xTb):
        for h in range(16):
            pt, pb = c.fm(xTt, xTb, h * 128, 128)
            c.rope_store(pt, pb, 128, K1[h, :, tcols(t)], db["K1"], 1.0, h % 2)

    gemm_pass(W0_TILES, X1, db["X1"], [(w_in1, 2048, 2048)], body_C1, save_xT=XT1, rope_dim=128, cs_t=cs128w)

    def body_C2(c, ti, t, xTt, xTb):
        for half in range(2):
            c.tm(xTt, xTb, 16, c.Wb, c.Wbs, half * 1024, 1024,
                 [(V1[half * 8 + j, t * 512:(t + 1) * 512, :], db["V1"]) for j in range(8)], ti * 2 + half)

    gemm_pass(W0_TILES, None, None, [(w_in1, 4096, 2048)], body_C2, load_xT=XT1, use_tm=1024)

    def body_C3(c, ti, t, xTt, xTb):
        for h in range(16):
            pt, pb = c.fm(xTt, xTb, h * 128, 128)
            c.rope_store(pt, pb, 128, Q1[h, :, tcols(t)], db["Q1"], 128.0 ** -0.5, h % 2)

    SH_ROWS = {14: (X1H, 0, db["X1H"]), 15: (X1H, 512, db["X1H"])}
    gemm_pass(OUT_TILES, None, None, [(w_in1, 0, 2048)], body_C3, load_xT=XT1, rope_dim=128, cs_t=cs128w, row_tiles=SH_ROWS)

    def body_C4(c, ti, t, xTt, xTb):
        for h in range(16):
            pt, pb = c.fm(xTt, xTb, h * 128, 128)
            c.evac_store(pt, pb, 128, G1[h, :, tcols(t)], db["G1"], func=AF.Silu)

    gemm_pass(OUT_TILES, None, None, [(w_in1, 6144, 2048)], body_C4, load_xT=XT1, row_tiles=SH_ROWS)

    dil_ctx = {}

    def dil_alloc(A_):
        masks = {}
        stgs = [A_.tile(f"dstg{i}", [512], F32) for i in range(4)]
        for j in range(52):
            stg, stgb = stgs[j % 4]
            if j < 20:
                m, mb = A_.tile(f"dm{j}", [512], BF16)
                load('sp', stg, stgb, dilmask[j, :, :])
                masks[j] = (m, mb)
            else:
                m, mb = A_.tile(f"sm{j - 20}", [512], BF16)
                load('sp', stg, stgb, smask[j - 20, :, :])
                masks[('s', (j - 20) // 16, (j - 20) % 16)] = (m, mb)
            if (j // 4) % 2 == 0:
                S.op('dve', lambda E, m=m, st_=stg: E.tensor_copy(out=m, in_=st_), reads=[stgb], writes=[mb])
            else:
                S.op('act', lambda E, m=m, st_=stg: E.copy(out=m, in_=st_), reads=[stgb], writes=[mb])
        kb, kbb = A_.tile("kb", [40], F32)
        load('sp', kb, kbb, kbias[:, :])
        dil_ctx.update(kb=kb, kbb=kbb)
        return masks

    dil_units = []

    class LazyBias:
        def __init__(self, kt):
            self.kt = kt

    for h in range(16):
        qts = []
        for qt in P_OUT:
            keys = [(kt, kt - 4 * qt + 8, LazyBias(kt)) for kt in range(4 * qt - 8, 4 * qt + 12)]
            qts.append(dict(Q=[(Q1[h, :, tcols(qt)], db["Q1"], 128)], G=(G1[h, :, tcols(qt)], db["G1"]), keys=keys,
                            out=(YG1[h * 128:(h + 1) * 128, tcols(qt)], db["YG1"])))
        dil_units.append(dict(K=[(K1[h, :, 0:5120], db["K1"], 128)], nkt=40, V=(V1[h, 0:5120, :], db["V1"]),
                              prep=None, qtiles=qts))
        qts = []
        for ql in range(2):
            wt = 14 + ql
            keys = [(kt, ('s', ql, kt), None) for kt in range(16)]
            qts.append(dict(Q=[(Q1[h, :, tcols(wt)], db["Q1"], 128)], G=(G1[h, :, tcols(wt)], db["G1"]), keys=keys,
                            out=(YG1[h * 128:(h + 1) * 128, tcols(wt)], db["YG1"])))
        dil_units.append(dict(K=[(K1[h, :, 5120:7168], db["K1"], 128)], nkt=16, V=(V1[h, 5120:7168, :], db["V1"]),
                              prep=None, qtiles=qts))

    def dil_fix(masks):
        pass

    def dil_prep(masks):
        kb, kbb = dil_ctx["kb"], dil_ctx["kbb"]
        for u in dil_units:
            for q in u["qtiles"]:
                q["keys"] = [(kt, mk, ((kb[:, b.kt:b.kt + 1], kbb) if isinstance(b, LazyBias) else b))
                             for (kt, mk, b) in q["keys"]]
    dil_units[0]["prep"] = dil_prep
    attn_phase(dil_units, dil_alloc)

    outproj_phase(YG1, wout1, ln1g, ln1b, OUT_TILES,
                  lambda t: ((X1H[(t - 14) * 512:(t - 13) * 512, :], db["X1H"]) if t >= 14
                             else (X1[t * 512:(t + 1) * 512, :], db["X1"])),
                  lambda ti, t: (y_own[ti * 512:(ti + 1) * 512, :], db["y_own"]))
    S.barrier()
    S.emit()
    stack.close()
    return nc


def _dil_mult(diff):
    c = (np.abs(diff) <= 64).astype(np.float32)
    c += ((diff % 4 == 0) & (np.abs(diff) <= 256)).astype(np.float32)
    c += ((diff % 16 == 0) & (np.abs(diff) <= 1024)).astype(np.float32)
    return c


def _rope_tab(half, pos):
    inv = (10000.0 ** (-np.arange(half, dtype=np.float32) / np.float32(half))).astype(np.float32)
    ang = (pos.astype(np.float32)[None, :] * inv[:, None]).astype(np.float32)
    cos = np.cos(ang).astype(np.float32)
    sin = np.sin(ang).astype(np.float32)
    return np.ascontiguousarray(np.stack([np.concatenate([cos, cos], 0), np.concatenate([sin, sin], 0)], 0))


def _host_consts():
    c = {}
    for half in (32, 64):
        n = 2 * half
        R = np.zeros((n, n), np.float32)
        for m in range(half):
            R[m + half, m] = -1.0
            R[m, m + half] = 1.0
        c["r64" if half == 32 else "r128"] = R
    c["ident"] = np.eye(128, dtype=np.float32)
    dm = np.zeros((20, 128, 512), np.float32)
    i = np.arange(128)[:, None]
    j = np.arange(512)[None, :]
    for r in range(20):
        dm[r] = _dil_mult((r - 8) * 128 + i - j)
    c["dilmask"] = dm
    W = np.zeros((64, 64), np.float32)
    for cc in range(64):
        cs = min(max(cc - 8, 0), 48)
        W[cs:cs + 16, cc] = 1.0
    c["nawin"] = np.ascontiguousarray(np.concatenate([W, W], 0))
    c["cs64s"] = _rope_tab(32, np.concatenate([np.arange(8192), np.arange(2048)]))
    return c


_NC_CACHE = {}


def kernel(x_prompt, x_sample, ab_w_in, ab_rpb, ab_q_norm_g, ab_w_q_up, ab_kv_norm_g, ab_w_kv_up,
           ab_w_out, ab_ln_g, ab_ln_b, c_w_in, c_w_out, c_ln_g, c_ln_b):
    f = lambda a: np.ascontiguousarray(np.asarray(a, dtype=np.float32))
    xp, xs = f(x_prompt), f(x_sample)
    wq = f(ab_w_q_up).reshape(512, 8, 192)
    wkv = f(ab_w_kv_up).reshape(512, 8, 256)
    rpb = f(ab_rpb)
    shared = _host_consts()
    shared["w_in0"] = f(ab_w_in)
    shared["wq_r"] = np.ascontiguousarray(np.concatenate([wq[:, :, :128].reshape(512, 1024), wq[:, :, 128:].reshape(512, 512)], 1))
    shared["wkv_r"] = np.ascontiguousarray(np.concatenate([wkv[:, :, :128].reshape(512, 1024), wkv[:, :, 128:].reshape(512, 1024)], 1))
    shared["qng"] = np.ascontiguousarray(f(ab_q_norm_g).reshape(4, 128).T)
    shared["kvng"] = np.ascontiguousarray(f(ab_kv_norm_g).reshape(4, 128).T)
    Apad = np.zeros((8, 15, 31 + 96), np.float32)
    Apad[:, :, 48:79] = rpb
    jj = np.arange(64)[:, None] - np.arange(64)[None, :] + 63
    tz = Apad.reshape(120, 127)[:, jj]
    tz = np.ascontiguousarray(tz.transpose(1, 0, 2)).reshape(64, 120 * 64)
    shared["tzin"] = np.ascontiguousarray(np.concatenate([tz, tz], 0))
    shared["wout0"] = f(ab_w_out)
    shared["ln0g"] = f(ab_ln_g).reshape(1, D)
    shared["ln0b"] = f(ab_ln_b).reshape(1, D)
    shared["w_in1"] = f(c_w_in)
    shared["wout1"] = f(c_w_out)
    shared["ln1g"] = f(c_ln_g).reshape(1, D)
    shared["ln1b"] = f(c_ln_b).reshape(1, D)
    if "nc" not in _NC_CACHE:
        _NC_CACHE["nc"] = build_program()
    nc = _NC_CACHE["nc"]
    in_maps = []
    for c in range(NCORES):
        b, g, sb = c // 4, c % 4, c // 2
        m = dict(shared)
        m["xseq"] = np.ascontiguousarray(np.concatenate([xp[b], xs[sb]], 0))
        w0 = 2048 * g - 1536
        pos = np.arange(w0, w0 + 5120)
        valid = (pos >= 0) & (pos < 8192)
        xw = np.zeros((5120, D), np.float32)
        xw[valid] = xp[b][pos[valid]]
        m["xwin"] = np.ascontiguousarray(np.concatenate([xw, xs[sb]], 0))
        hf = c % 2
        wpos = np.concatenate([np.clip(pos, 0, 8191), np.arange(2048)])
        m["cs64w"] = _rope_tab(32, wpos)
        m["cs128w"] = _rope_tab(64, np.concatenate([wpos, np.arange(1024 * hf, 1024 * hf + 1024)]))
        hi = np.zeros((128, 8), np.int32)
        hi[:, 0] = (5120 + 1024 * hf) // 8 + np.arange(128)
        m["hidx"] = hi
        sm = np.zeros((2, 16, 128, 512), np.float32)
        for ql in range(2):
            for kt in range(16):
                rel = kt - 4 * (2 * hf + ql) + 8
                if 0 <= rel < 20:
                    sm[ql, kt] = shared["dilmask"][rel]
        m["smask"] = np.ascontiguousarray(sm.reshape(32, 128, 512))
        kb = np.where(valid, 0.0, NEG).astype(np.float32).reshape(40, 128).T
        m["kbias"] = np.ascontiguousarray(kb)
        s_top, s_bot = float(g == 0), float(g == 3)
        m["sel"] = np.ascontiguousarray(np.tile(np.array([[s_top, 1 - s_top, s_bot, 1 - s_bot]], np.float32), (128, 1)))
        in_maps.append(m)
    res = run_bass_kernel_spmd(nc, in_maps, core_ids=list(range(NCORES)))
    y_prompt = np.zeros((2, 8192, D), np.float32)
    y_sample = np.zeros((4, 2048, D), np.float32)
    for c in range(NCORES):
        b, g, sb, hf = c // 4, c % 4, c // 2, c % 2
        y = np.asarray(res.results[c]["y_own"], dtype=np.float32)
        y_prompt[b, 2048 * g:2048 * (g + 1)] = y[0:2048]
        y_sample[sb, 1024 * hf:1024 * (hf + 1)] = y[2048:3072]
    return (y_prompt, y_sample)
```

```python
import contextlib
import numpy as np
import concourse.bass as bass
import concourse.mybir as mybir
from concourse.bass_utils import run_bass_kernel_spmd

F32, BF16, I32 = mybir.dt.float32, mybir.dt.bfloat16, mybir.dt.int32
ALU = mybir.AluOpType
AF = mybir.ActivationFunctionType

NCORES = 8
D = 2048
WT = 7168
ST = 10240
P_W0 = list(range(1, 9))
P_OUT = list(range(3, 7))
S_WT = list(range(10, 14))
W0_TILES = P_W0 + S_WT
SH_TILES = [14, 15]
OUT_TILES = P_OUT + SH_TILES
NOUT = len(OUT_TILES) * 512
WTX = WT + 1024
NEG = -30000.0
ALPHA = 4.0 ** 0.25
LN_EPS = 1e-5
RMS_EPS = 1e-6
SEM_ROT = 12000
ARENA_WORDS = 50176


class Buf:
    def __init__(self, name):
        self.name = name
        self.writes = {}
        self.reads = {}
        self.w_is_dma = False
        self.dsems = {}


def _merge(dst, src):
    for k, v in src.items():
        if dst.get(k, 0) < v:
            dst[k] = v


class Sched:
    ENG = ('pe', 'act', 'dve', 'pool', 'sp')

    def __init__(self, nc, stack):
        self.nc = nc
        self.stack = stack
        self.prog = {e: [] for e in self.ENG}
        self.ecount = {e: 0 for e in self.ENG}
        self.esems = {e: [] for e in self.ENG}
        self.waited = {e: {} for e in self.ENG}
        self.bufs = []
        self.nsem = 0
        self.semh = {}
        self.free_dsems = {}
        self.phase_bufs = []

    def newsem(self, name):
        self.nsem += 1
        h = self.stack.enter_context(self.nc.semaphore(f"{name}_{self.nsem}"))
        key = self.nsem
        self.semh[key] = h
        return key

    def buf(self, name):
        b = Buf(name)
        self.bufs.append(b)
        return b

    def _tok(self, e):
        k = self.ecount[e]
        idx = k // SEM_ROT
        while len(self.esems[e]) <= idx:
            self.esems[e].append(self.newsem(f"p{e}"))
        self.ecount[e] += 1
        return (self.esems[e][idx], k % SEM_ROT + 1)

    def _wait(self, e, need):
        for sk, val in need.items():
            if self.waited[e].get(sk, 0) >= val:
                continue
            self.waited[e][sk] = val
            h = self.semh[sk]
            self.prog[e].append(lambda E, h=h, val=val: E.wait_ge(h, val))

    def op(self, e, fn, reads=(), writes=()):
        need = {}
        for b in reads:
            _merge(need, b.writes)
        for b in writes:
            _merge(need, b.reads)
            _merge(need, b.writes)
        if e == 'pe':
            for sk in self.esems['pe']:
                need.pop(sk, None)
        self._wait(e, need)
        tok = self._tok(e)
        h = self.semh[tok[0]]
        self.prog[e].append(lambda E, fn=fn, h=h: fn(E).then_inc(h, 1))
        for b in reads:
            if b.reads.get(tok[0], 0) < tok[1]:
                b.reads[tok[0]] = tok[1]
        for b in writes:
            b.writes = {tok[0]: tok[1]}
            b.reads = {}
            b.w_is_dma = False

    def dma(self, q, fn, sb, reads=(), writes=()):
        need = {}
        for b in reads:
            _merge(need, b.writes)
        for b in writes:
            _merge(need, b.reads)
            if not b.w_is_dma:
                _merge(need, b.writes)
        self._wait(q, need)
        kind = 'sw' if q == 'pool' else 'hw'
        if kind not in sb.dsems:
            fl = self.free_dsems.setdefault(kind, [])
            sb.dsems[kind] = list(fl.pop()) if fl else [self.newsem("d" + kind), 0]
        ent = sb.dsems[kind]
        ent[1] += 16
        tok = (ent[0], ent[1])
        h = self.semh[tok[0]]
        self.prog[q].append(lambda E, fn=fn, h=h: fn(E).then_inc(h, 16))
        for b in reads:
            if b.reads.get(tok[0], 0) < tok[1]:
                b.reads[tok[0]] = tok[1]
        for b in writes:
            if b.reads or not b.w_is_dma:
                b.writes = {}
            b.writes[tok[0]] = tok[1]
            b.reads = {}
            b.w_is_dma = True

    def barrier(self):
        need = {}
        for e in self.ENG:
            if self.ecount[e] > 0:
                k = self.ecount[e] - 1
                need[self.esems[e][k // SEM_ROT]] = k % SEM_ROT + 1
        for b in self.bufs:
            for ent in b.dsems.values():
                if need.get(ent[0], 0) < ent[1]:
                    need[ent[0]] = ent[1]
            for d in (b.writes, b.reads):
                _merge(need, d)
        for e in self.ENG:
            self._wait(e, dict(need))

    def emit(self):
        nc = self.nc
        with nc.Block() as block:
            @block.tensor
            def _(E):
                for th in self.prog['pe']:
                    th(E)

            @block.scalar
            def _(E):
                for th in self.prog['act']:
                    th(E)

            @block.vector
            def _(E):
                for th in self.prog['dve']:
                    th(E)

            @block.gpsimd
            def _(E):
                for th in self.prog['pool']:
                    th(E)

            @block.sync
            def _(E):
                for th in self.prog['sp']:
                    th(E)


class Arena:
    def __init__(self, ap, sched):
        self.ap = ap
        self.s = sched
        self.off = 0
        self.live = []

    def reset(self):
        self.off = 0
        for b in self.live:
            for kind, ent in b.dsems.items():
                self.s.free_dsems.setdefault(kind, []).append((ent[0], ent[1]))
            b.dsems = {}
            if b in self.s.bufs:
                self.s.bufs.remove(b)
        self.live = []

    def tile(self, name, free_shape, dt):
        n = int(np.prod(free_shape))
        words = n if dt != BF16 else (n + 1) // 2
        assert self.off + words <= ARENA_WORDS, (name, self.off, words)
        a = self.ap[:, self.off:self.off + words]
        self.off += words
        if dt == BF16:
            a = a.bitcast(BF16)
        elif dt == I32:
            a = a.bitcast(I32)
        if len(free_shape) == 2:
            a = a.rearrange("p (a b) -> p a b", a=free_shape[0])
        elif len(free_shape) == 3:
            a = a.rearrange("p (a b c) -> p a b c", a=free_shape[0], b=free_shape[1])
        b = self.s.buf(name)
        self.live.append(b)
        return a, b


class Ctx:
    pass


def build_program():
    nc = bass.Bass("TRN2", target_bir_lowering=False)
    stack = contextlib.ExitStack()
    S = Sched(nc, stack)

    def din(name, shape, dt=F32):
        return nc.dram_tensor(name, list(shape), dt, kind="ExternalInput")

    xseq = din("xseq", [ST, D])
    xwin = din("xwin", [WT, D])
    w_in0 = din("w_in0", [D, 6208])
    wq_r = din("wq_r", [512, 1536])
    wkv_r = din("wkv_r", [512, 2048])
    qng = din("qng", [128, 4])
    kvng = din("kvng", [128, 4])
    tzin = din("tzin", [128, 120 * 64])
    wout0 = din("wout0", [D, D])
    ln0g = din("ln0g", [1, D])
    ln0b = din("ln0b", [1, D])
    w_in1 = din("w_in1", [D, 8192])
    wout1 = din("wout1", [D, D])
    ln1g = din("ln1g", [1, D])
    ln1b = din("ln1b", [1, D])
    cs64s = din("cs64s", [2, 64, ST])
    cs64w = din("cs64w", [2, 64, WT])
    cs128w = din("cs128w", [2, 128, WTX])
    r64 = din("r64", [64, 64])
    r128 = din("r128", [128, 128])
    ident_in = din("ident", [128, 128])
    dilmask = din("dilmask", [20, 128, 512])
    nawin = din("nawin", [128, 64])
    kbias = din("kbias", [128, 40])
    sel_in = din("sel", [128, 4])
    hidx_in = din("hidx", [128, 8], I32)
    smask = din("smask", [32, 128, 512])
    y_own = nc.dram_tensor("y_own", [NOUT, D], F32, kind="ExternalOutput")

    db = {}

    def dscr(name, shape, dt):
        t = nc.dram_tensor(name, list(shape), dt)
        db[name] = S.buf(name)
        return t

    db["y_own"] = S.buf("y_own")
    XTW = dscr("XTW", [128, 16 * WT], BF16)
    XT1 = dscr("XT1", [128, 16 * WT], BF16)
    KA = dscr("KA", [8, 128, WT], BF16)
    QA = dscr("QA", [8, 128, WT], BF16)
    GA = dscr("GA", [8, 128, WT], BF16)
    GB = dscr("GB", [8, 128, WT], BF16)
    QM0 = dscr("QM0", [8, 128, WT], BF16)
    QM1 = dscr("QM1", [8, 64, WT], BF16)
    VA = dscr("VA", [8, WT, 128], BF16)
    KM0 = dscr("KM0", [8, 128, ST], BF16)
    KM1 = dscr("KM1", [64, ST], BF16)
    VM = dscr("VM", [8, ST, 128], BF16)
    YG0 = dscr("YG0", [D, WT], BF16)
    X1 = dscr("X1", [WT, D], F32)
    K1 = dscr("K1", [16, 128, WT], BF16)
    Q1 = dscr("Q1", [16, 128, WTX], BF16)
    G1 = dscr("G1", [16, 128, WTX], BF16)
    V1 = dscr("V1", [16, WT, 128], BF16)
    YG1 = dscr("YG1", [D, WTX], BF16)
    X1H = dscr("X1H", [1024, D], F32)

    arena_t = stack.enter_context(nc.sbuf_tensor("arena", [128, ARENA_WORDS], F32))
    A = Arena(arena_t, S)
    psum = []
    for i in range(8):
        pt = stack.enter_context(nc.psum_tensor(f"ps{i}", [128, 512], F32))
        psum.append((pt, S.buf(f"ps{i}")))
    pctr = [0]

    def nextbank(lo=0, hi=8):
        i = lo + pctr[0] % (hi - lo)
        pctr[0] += 1
        return psum[i]

    rr = [0]

    def rot(engs):
        rr[0] += 1
        return engs[rr[0] % len(engs)]

    def load(q, dst_ap, dst_b, src_ap, src_b=None):
        S.dma(q, lambda E: E.dma_start(out=dst_ap, in_=src_ap), dst_b,
              reads=([src_b] if src_b else []), writes=[dst_b])

    def store(q, dst_ap, dst_b, src_ap, src_b):
        S.dma(q, lambda E: E.dma_start(out=dst_ap, in_=src_ap), src_b, reads=[src_b], writes=[dst_b])

    def load_cast_weight(dst, dst_b, dcol0, src_dram, rows, col0, cols, stg):
        for k in range(rows // 128):
            c0 = 0
            while c0 < cols:
                cw = min(2048, cols - c0)
                wl_ctr[0] += 1
                st, stb = stg[wl_ctr[0] % len(stg)]
                load('sp', st[:, 0:cw], stb, src_dram[k * 128:(k + 1) * 128, col0 + c0:col0 + c0 + cw])
                e = 'act' if (wl_ctr[0] // len(stg)) % 2 == 0 else 'dve'
                o = dst[:, k, dcol0 + c0:dcol0 + c0 + cw]
                if e == 'act':
                    S.op('act', lambda E, o=o, i=st[:, 0:cw]: E.copy(out=o, in_=i), reads=[stb], writes=[dst_b])
                else:
                    S.op(e, lambda E, o=o, i=st[:, 0:cw]: E.tensor_copy(out=o, in_=i), reads=[stb], writes=[dst_b])
                c0 += cw

    wl_ctr = [0]

    def load_weight_coltiles(dst, dst_bufs, dcol0, src_dram, col0, cols, stg):
        g0 = 0
        while g0 < cols:
            gw = min(512, cols - g0)
            for k0 in range(0, 16, 4):
                wl_ctr[0] += 1
                st, stb = stg[wl_ctr[0] % len(stg)]
                stv = st[:, 0:4 * gw].rearrange("p (k c) -> p k c", k=4)
                load('sp', stv, stb,
                     src_dram[k0 * 128:(k0 + 4) * 128, col0 + g0:col0 + g0 + gw].rearrange("(k p) c -> p k c", p=128))
                o = dst[:, k0:k0 + 4, dcol0 + g0:dcol0 + g0 + gw]
                buf = dst_bufs[(dcol0 + g0) // 512]
                if (wl_ctr[0] // len(stg)) % 2 == 0:
                    S.op('act', lambda E, o=o, i=stv: E.copy(out=o, in_=i), reads=[stb], writes=[buf])
                else:
                    S.op('dve', lambda E, o=o, i=stv: E.tensor_copy(out=o, in_=i), reads=[stb], writes=[buf])
            g0 += gw

    def gemm_pass(tiles, src, src_b, wblocks, body, save_xT=None, load_xT=None, use_tm=0, use_lat=None,
                  rope_dim=0, cs_t=None, post=None, row_tiles=None):
        S.barrier()
        A.reset()
        c = Ctx()
        bank_hi = 7 if use_lat is not None else 8
        ncols = sum(w[2] for w in wblocks)
        c.Wb, c.Wb_b = A.tile("Wb", [16, ncols], BF16)
        stg = [A.tile(f"stg{i}", [2048], F32) for i in range(2)]
        inits = []
        row_tiles = row_tiles or {}
        if load_xT is None or row_tiles:
            identb, identb_b = A.tile("identb", [128], BF16)
            xb = [A.tile(f"xb{i}", [2048], BF16) for i in range(4)]
            def init_ident():
                load('sp', stg[0][0][:, 0:128], stg[0][1], ident_in[:, :])
                S.op('dve', lambda E: E.tensor_copy(out=identb, in_=stg[0][0][:, 0:128]), reads=[stg[0][1]], writes=[identb_b])
            inits.append(init_ident)
        xT = [A.tile(f"xT{i}", [16, 512], BF16) for i in range(2)]
        outs = [A.tile(f"o{i}", [512], BF16) for i in range(8)]
        c.Wbs = [S.buf(f"Wbc{j}") for j in range((ncols + 511) // 512)]
        A.live.extend(c.Wbs)

        def init_weights():
            dc = 0
            for (wt_, c0_, n_) in wblocks:
                load_weight_coltiles(c.Wb, c.Wbs, dc, wt_, c0_, n_, stg)
                dc += n_
        inits.append(init_weights)
        if rope_dim:
            cosT = [A.tile(f"cos{i}", [512], F32) for i in range(2)]
            sinT = [A.tile(f"sin{i}", [512], F32) for i in range(2)]
            ra = [A.tile(f"ra{i}", [512], F32) for i in range(2)]
            rt1 = [A.tile(f"rt1{i}", [512], F32) for i in range(2)]
            rt2 = [A.tile(f"rt2{i}", [512], F32) for i in range(2)]
            rmat, rmat_b = A.tile("rmat", [128], F32)
            inits.append(lambda: load('sp', rmat[0:rope_dim, 0:rope_dim], rmat_b, (r64 if rope_dim == 64 else r128)[:, :]))
        if use_tm:
            vouts = [A.tile(f"vo{i}", [4, use_tm], BF16) for i in range(2)]
        if use_lat is not None:
            wup_d, nup, g_d = use_lat
            c.wup, c.wup_b = A.tile("wup", [4, nup], BF16)
            gg, gg_b = A.tile("gg", [4], F32)
            onesf, onesf_b = A.tile("onesf", [128], F32)

            def init_lat():
                load_cast_weight(c.wup, c.wup_b, 0, wup_d, 512, 0, nup, stg)
                load('sp', gg, gg_b, g_d[:, :])
                S.op('pool', lambda E: E.memset(onesf, 1.0), writes=[onesf_b])
            inits.append(init_lat)
            latT = [A.tile(f"lat{i}", [4, 512], F32) for i in range(2)]
            sqT = [A.tile(f"sq{i}", [4, 512], BF16) for i in range(2)]
            ones16, ones16_b = A.tile("ones16", [128], BF16)
            inits.append(lambda: S.op('dve', lambda E: E.memset(ones16, 1.0), writes=[ones16_b]))
            rstdT = [A.tile(f"rstd{i}", [512], F32) for i in range(2)]
            latnT = [A.tile(f"latn{i}", [4, 512], BF16) for i in range(2)]
            c.latnT = latnT
            c.lat_tail = {}
        octr = [0]

        def nxt_out():
            octr[0] += 1
            return outs[octr[0] % len(outs)]

        def fm(xTt, xTb, c0, M):
            pt, pb = nextbank(0, bank_hi)
            for k in range(16):
                S.op('pe', lambda E, o=pt[0:M, :], l=c.Wb[:, k, c0:c0 + M], r=xTt[:, k, :], st=(k == 0), sp=(k == 15):
                     E.matmul(o, l, r, start=st, stop=sp), reads=[c.Wbs[c0 // 512], xTb], writes=[pb])
            return pt, pb

        def fm_up(c0, M, slot):
            pt, pb = nextbank(0, bank_hi)
            latn, latn_b = c.latnT[slot]
            for k in range(4):
                S.op('pe', lambda E, o=pt[0:M, :], l=c.wup[:, k, c0:c0 + M], r=latn[:, k, :], st=(k == 0), sp=(k == 3):
                     E.matmul(o, l, r, start=st, stop=sp), reads=[c.wup_b, latn_b], writes=[pb])
            return pt, pb

        def evac_store(pt, pb, M, dst_ap, dst_b, func=None, scale=1.0):
            o, ob = nxt_out()
            fn = func if func is not None else AF.Copy
            S.op('act', lambda E, o=o[0:M, :], i=pt[0:M, :]: E.activation(out=o, in_=i, func=fn, scale=scale),
                 reads=[pb], writes=[ob])
            store('pool', dst_ap, dst_b, o[0:M, :], ob)

        c.rope_pending = []

        def rope_store(pt, pb, M, dst_ap, dst_b, scale, slot):
            a, ab = ra[slot]
            t1, t1b = rt1[slot]
            t2, t2b = rt2[slot]
            cb, sb_ = c.cb, c.sb
            S.op('act', lambda E: E.activation(out=a[0:M, :], in_=pt[0:M, :], func=AF.Copy, scale=scale),
                 reads=[pb], writes=[ab])

            def tail():
                p2, p2b = nextbank(0, bank_hi)
                S.op('pe', lambda E: E.matmul(p2[0:M, :], rmat[0:M, 0:M], a[0:M, :], start=True, stop=True),
                     reads=[rmat_b, ab], writes=[p2b])
                S.op('pool', lambda E: E.tensor_tensor(out=t1[0:M, :], in0=a[0:M, :], in1=cb[0][0:M, :], op=ALU.mult),
                     reads=[ab, cb[1]], writes=[t1b])
                S.op('dve', lambda E: E.tensor_tensor(out=t2[0:M, :], in0=p2[0:M, :], in1=sb_[0][0:M, :], op=ALU.mult),
                     reads=[p2b, sb_[1]], writes=[t2b])
                o, ob = nxt_out()
                S.op('dve', lambda E: E.tensor_tensor(out=o[0:M, :], in0=t1[0:M, :], in1=t2[0:M, :], op=ALU.add),
                     reads=[t1b, t2b], writes=[ob])
                store('pool', dst_ap, dst_b, o[0:M, :], ob)
            c.rope_pending.append(tail)
            while len(c.rope_pending) > 1:
                c.rope_pending.pop(0)()

        def rope_flush():
            while c.rope_pending:
                c.rope_pending.pop(0)()

        def tm(lhs, lhsb, nk, W, Wb_, c0, ncols, dsts, vslot):
            vo, vob = vouts[vslot % 2]
            for sub in range(4):
                for g0 in range(0, ncols, 512):
                    gw = min(512, ncols - g0)
                    pt, pb = nextbank(0, bank_hi)
                    for k in range(nk):
                        S.op('pe', lambda E, o=pt[:, 0:gw], l=lhs[:, k, sub * 128:(sub + 1) * 128],
                             r=W[:, k, c0 + g0:c0 + g0 + gw], st=(k == 0), sp=(k == nk - 1):
                             E.matmul(o, l, r, start=st, stop=sp),
                             reads=[lhsb] + ([Wb_[(c0 + g0) // 512]] if isinstance(Wb_, list) else [Wb_]),
                             writes=[pb])
                    S.op('dve', lambda E, o=vo[:, sub, g0:g0 + gw], i=pt[:, 0:gw]: E.tensor_copy(out=o, in_=i),
                         reads=[pb], writes=[vob])
            for j, (dap, dbuf) in enumerate(dsts):
                store('pool', dap.rearrange("(t p) d -> p t d", p=128), dbuf, vo[:, :, j * 128:(j + 1) * 128], vob)

        def lat_gemm(xTt, xTb, cbase, slot):
            pss, pssb = psum[7]
            lat, lat_b = latT[slot]
            sq, sq_b = sqT[slot]
            rstd, rstd_b = rstdT[slot]
            latn, latn_b = latnT[slot]

            def ssq(cc):
                S.op('pe', lambda E, r=sq[:, cc, :], st=(cc == 0), sp=(cc == 3), p_=pss[:, :]:
                     E.matmul(p_, ones16, r, start=st, stop=sp), reads=[ones16_b, sq_b], writes=[pssb])
            for cc in range(4):
                pt, pb = fm(xTt, xTb, cbase + cc * 128, 128)
                S.op('act', lambda E, o=lat[:, cc, :], i=pt[:, :]: E.copy(out=o, in_=i), reads=[pb], writes=[lat_b])
                S.op('dve', lambda E, o=sq[:, cc, :], i=lat[:, cc, :]: E.tensor_tensor(out=o, in0=i, in1=i, op=ALU.mult),
                     reads=[lat_b], writes=[sq_b])
                if cc > 0:
                    ssq(cc - 1)

            def tail():
                ssq(3)
                S.op('dve', lambda E, p_=pss[:, :]: E.tensor_scalar(out=rstd, in0=p_, scalar1=1.0 / 512, scalar2=RMS_EPS,
                                                                    op0=ALU.mult, op1=ALU.add), reads=[pssb], writes=[rstd_b])
                S.op('act', lambda E: E.activation(out=rstd, in_=rstd, func=AF.Ln), reads=[rstd_b], writes=[rstd_b])
                S.op('act', lambda E: E.activation(out=rstd, in_=rstd, func=AF.Exp, scale=-0.5), reads=[rstd_b], writes=[rstd_b])
                for cc in range(4):
                    S.op('dve', lambda E, o=latn[:, cc, :], i=lat[:, cc, :], g=gg[:, cc:cc + 1]:
                         E.scalar_tensor_tensor(out=o, in0=i, scalar=g, in1=rstd, op0=ALU.mult, op1=ALU.mult),
                         reads=[lat_b, gg_b, rstd_b], writes=[latn_b])
            c.lat_tail[slot] = tail

        def lat_finish(slot):
            c.lat_tail.pop(slot)()

        while len(stg) < 4 and ARENA_WORDS - A.off >= 2048 + 64:
            stg.append(A.tile(f"stg{len(stg)}", [2048], F32))
        for f_ in inits:
            f_()
        c.fm, c.fm_up, c.evac_store, c.rope_store, c.tm = fm, fm_up, evac_store, rope_store, tm
        c.lat_gemm, c.lat_finish, c.rope_flush = (lat_gemm, lat_finish, rope_flush) if use_lat is not None else (None, None, rope_flush)
        for ti, t in enumerate(tiles):
            xTt, xTb = xT[ti % 2]
            tok0 = t * 512
            if rope_dim:
                c.cb, c.sb = cosT[ti % 2], sinT[ti % 2]
                load('sp', c.cb[0][0:rope_dim, :], c.cb[1], cs_t[0, :, tok0:tok0 + 512])
                load('sp', c.sb[0][0:rope_dim, :], c.sb[1], cs_t[1, :, tok0:tok0 + 512])
            if load_xT is not None and t not in row_tiles:
                load('sp', xTt, xTb, load_xT.ap().rearrange("p (k t) -> p k t", k=16)[:, :, tok0:tok0 + 512],
                     db[load_xT.name])
            else:
                if t in row_tiles:
                    rsrc_, rrow0, rbuf_ = row_tiles[t]
                else:
                    rsrc_, rrow0, rbuf_ = src, tok0, src_b
                for sub in range(4):
                    xfi, xfb = stg[sub % len(stg)]
                    xbi, xbb = xb[sub]
                    load('sp', xfi, xfb, rsrc_[rrow0 + sub * 128:rrow0 + (sub + 1) * 128, :], rbuf_)
                    S.op('act', lambda E, o=xbi, i=xfi: E.copy(out=o, in_=i), reads=[xfb], writes=[xbb])
                    for half in range(2):
                        pt, pb = nextbank(0, bank_hi)
                        ptb = pt[:, :].bitcast(BF16)
                        for j in range(8):
                            k = half * 8 + j
                            S.op('pe', lambda E, o=ptb[:, j * 128:(j + 1) * 128], i=xbi[:, k * 128:(k + 1) * 128]:
                                 E.transpose(o, i, identb), reads=[xbb, identb_b], writes=[pb])
                        S.op('dve', lambda E, o=xTt[:, half * 8:(half + 1) * 8, sub * 128:(sub + 1) * 128],
                             i=ptb.rearrange("p (a b) -> p a b", a=8): E.tensor_copy(out=o, in_=i),
                             reads=[pb], writes=[xTb])
                if save_xT is not None and t not in row_tiles:
                    store('pool', save_xT.ap().rearrange("p (k t) -> p k t", k=16)[:, :, tok0:tok0 + 512],
                          db[save_xT.name], xTt, xTb)
            body(c, ti, t, xTt, xTb)
        if post is not None:
            post(c)
        while c.rope_pending:
            c.rope_pending.pop(0)()

    def attn_phase(units, alloc_extra=None):
        S.barrier()
        A.reset()
        onesb, onesb_b = A.tile("onesb", [128], BF16)
        S.op('pool', lambda E: E.memset(onesb, 1.0), writes=[onesb_b])
        masks = alloc_extra(A) if alloc_extra else {}
        nK = max(len(u["K"]) for u in units)
        Kt = [[A.tile(f"K{i}_{b}", [8192], BF16) for i in range(nK)] for b in range(2)]
        Vt = [A.tile(f"V{b}", [64, 128], BF16) for b in range(2)]
        Qt = [[A.tile(f"Q{i}_{b}", [512], BF16) for i in range(nK)] for b in range(2)]
        Gt = [A.tile(f"G{b}", [512], BF16) for b in range(2)]
        Et = [A.tile(f"E{b}", [512], BF16) for b in range(6)]
        Pt = [A.tile(f"P{b}", [512], BF16) for b in range(6)]
        rec = [A.tile(f"rec{b}", [512], F32) for b in range(2)]
        yv = [A.tile(f"yv{b}", [512], F32) for b in range(2)]
        yo = [A.tile(f"yo{b}", [512], BF16) for b in range(2)]
        if any(u.get("dacc") for u in units):
            dacc = [A.tile(f"dacc{b}", [512], F32) for b in range(2)]
            onesf32, onesf32_b = A.tile("onesf32", [128], F32)
            S.op('pool', lambda E: E.memset(onesf32, 1.0), writes=[onesf32_b])
        qctr = 0
        ectr = 0
        LAG = 3
        pending = []

        def push(fn):
            pending.append(fn)
            while len(pending) > LAG:
                pending.pop(0)()

        def load_kv(ui):
            u = units[ui]
            kb = ui % 2
            nkt = u["nkt"]
            for i, (kap, kbuf, parts) in enumerate(u["K"]):
                load('sp', Kt[kb][i][0][0:parts, 0:nkt * 128], Kt[kb][i][1], kap, kbuf)
            vt, vtb = Vt[kb]
            load('sp', vt[:, 0:nkt, :], vtb, u["V"][0].rearrange("(t p) d -> p t d", p=128), u["V"][1])

        load_kv(0)
        step_units = [i for i, u_ in enumerate(units) if u_.get("prep_steps")]
        plan = {}
        prev = 0
        for j in step_units:
            steps = units[j]["prep_steps"](masks)
            if j == 0:
                na_ctx["force_dve"] = True
                for st_ in steps:
                    st_()
                na_ctx["force_dve"] = False
            else:
                slots = [(i, qi_) for i in range(prev, j) for qi_ in range(len(units[i]["qtiles"]))]
                per = -(-len(steps) // max(1, len(slots)))
                for si, key in enumerate(slots):
                    plan[key] = steps[si * per:(si + 1) * per]
                rest = steps[len(slots) * per:]
                if rest:
                    plan[slots[-1]] = plan[slots[-1]] + rest
            prev = j
        for ui, u in enumerate(units):
            if u.get("prep"):
                while pending:
                    pending.pop(0)()
                u["prep"](masks)
            kb = ui % 2
            vt, vtb = Vt[kb]
            nparts = len(u["K"])
            for qi, q in enumerate(u["qtiles"]):
                if qi == 1 and ui + 1 < len(units):
                    load_kv(ui + 1)
                qb = qctr % 2
                qctr += 1
                for i, (qap, qbuf, parts) in enumerate(q["Q"]):
                    load('sp', Qt[qb][i][0][0:parts, :], Qt[qb][i][1], qap, qbuf)
                gt, gtb = Gt[qb]
                load('sp', gt, gtb, q["G"][0], q["G"][1])
                po, pob = psum[4 + 2 * qb]
                pd, pdb = psum[5 + 2 * qb]
                kl = q["keys"]
                for ki, (kt, mk, bias) in enumerate(kl):
                    ps, psb = nextbank(0, 4)
                    for i, (kap, kbuf, parts) in enumerate(u["K"]):
                        S.op('pe', lambda E, o=ps[:, :], l=Kt[kb][i][0][0:parts, kt * 128:(kt + 1) * 128],
                             r=Qt[qb][i][0][0:parts, :], st=(i == 0), sp=(i == nparts - 1):
                             E.matmul(o, l, r, start=st, stop=sp),
                             reads=[Kt[kb][i][1], Qt[qb][i][1]], writes=[psb])
                    eb = ectr % 6
                    ectr += 1
                    pm, pmb = Pt[eb]
                    tgt, tgtb = (pm, pmb) if mk is None else Et[eb]
                    rds = [psb]
                    if bias is not None:
                        rds.append(bias[1])
                        S.op('act', lambda E, o=tgt, i=ps[:, :], b=bias[0]: E.activation(out=o, in_=i, func=AF.Exp, bias=b),
                             reads=rds, writes=[tgtb])
                    else:
                        S.op('act', lambda E, o=tgt, i=ps[:, :]: E.activation(out=o, in_=i, func=AF.Exp),
                             reads=rds, writes=[tgtb])
                    if mk is not None:
                        mt, mtb = masks[mk]
                        S.op('dve', lambda E, o=pm, a=tgt, m=mt: E.tensor_tensor(out=o, in0=a, in1=m, op=ALU.mult),
                             reads=[tgtb, mtb], writes=[pmb])

                    if u.get("dacc"):
                        ac, acb = dacc[qb]
                        if ki == 0:
                            S.op('dve', lambda E, ac=ac, pm=pm: E.tensor_copy(out=ac, in_=pm), reads=[pmb], writes=[acb])
                        else:
                            S.op('dve', lambda E, ac=ac, pm=pm: E.tensor_tensor(out=ac, in0=ac, in1=pm, op=ALU.add),
                                 reads=[pmb, acb], writes=[acb])

                    def pv(po=po, pob=pob, pd=pd, pdb=pdb, vt=vt, vtb=vtb, kt=kt, pm=pm, pmb=pmb,
                           st=(ki == 0), sp=(ki == len(kl) - 1), use_acc=bool(u.get("dacc")), qb=qb):
                        S.op('pe', lambda E: E.matmul(po[:, :], vt[:, kt, :], pm, start=st, stop=sp),
                             reads=[vtb, pmb], writes=[pob])
                        if not use_acc:
                            S.op('pe', lambda E: E.matmul(pd[:, :], onesb, pm, start=st, stop=sp),
                                 reads=[onesb_b, pmb], writes=[pdb])
                        elif sp:
                            ac, acb = dacc[qb]
                            S.op('pe', lambda E: E.matmul(pd[:, :], onesf32, ac, start=True, stop=True),
                                 reads=[onesf32_b, acb], writes=[pdb])
                    push(pv)

                def fin(qb=qb, po=po, pob=pob, pd=pd, pdb=pdb, gt=gt, gtb=gtb, q=q):
                    rc, rcb = rec[qb]
                    y, yb = yv[qb]
                    o, ob = yo[qb]
                    S.op('dve', lambda E: E.tensor_scalar_max(out=rc, in0=pd[:, :], scalar1=1e-30),
                         reads=[pdb], writes=[rcb])
                    S.op('act', lambda E: E.activation(out=rc, in_=rc, func=AF.Ln), reads=[rcb], writes=[rcb])
                    S.op('act', lambda E: E.activation(out=rc, in_=rc, func=AF.Exp, scale=-1.0), reads=[rcb], writes=[rcb])
                    S.op('dve', lambda E: E.tensor_tensor(out=y, in0=po[:, :], in1=rc, op=ALU.mult),
                         reads=[pob, rcb], writes=[yb])
                    S.op('pool', lambda E: E.tensor_tensor(out=o, in0=y, in1=gt, op=ALU.mult),
                         reads=[yb, gtb], writes=[ob])
                    store('pool', q["out"][0], q["out"][1], o, ob)
                push(fin)
                for st_ in plan.get((ui, qi), []):
                    st_()
        while pending:
            pending.pop(0)()

    def outproj_phase(YG, Wd, lg, lb_, tiles, resid, outdst):
        S.barrier()
        A.reset()
        Wb, Wb_b = A.tile("Wo", [16, 2048], BF16)
        stg = [A.tile(f"stg{i}", [2048], F32) for i in range(2)]
        Wbs = [S.buf(f"Woc{j}") for j in range(4)]
        A.live.extend(Wbs)
        load_weight_coltiles(Wb, Wbs, 0, Wd, 0, D, stg)
        gbc, gbc_b = A.tile("gbc", [2048], F32)
        bbc, bbc_b = A.tile("bbc", [2048], F32)
        load('sp', gbc, gbc_b, lg.ap()[0:1, :].partition_broadcast(128)[:, 0, :])
        load('sp', bbc, bbc_b, lb_.ap()[0:1, :].partition_broadcast(128)[:, 0, :])
        ygT = [A.tile(f"ygT{i}", [16, 512], BF16) for i in range(2)]
        tt_ = [A.tile(f"t{i}", [2048], F32) for i in range(2)]
        oo = [A.tile(f"oo{i}", [2048], F32) for i in range(2)]
        stats = [A.tile(f"st{i}", [4, 6], F32) for i in range(2)]
        mv = [A.tile(f"mv{i}", [2], F32) for i in range(2)]
        rs = [A.tile(f"rs{i}", [1], F32) for i in range(2)]
        nm_ = [A.tile(f"nm{i}", [1], F32) for i in range(2)]
        sctr = 0
        tails = []
        for ti, t in enumerate(tiles):
            yt, ytb = ygT[ti % 2]
            load('sp', yt, ytb, YG.ap().rearrange("(k p) t -> p k t", p=128)[:, :, t * 512:(t + 1) * 512], db[YG.name])
            rsrc, rsb_ = resid(t)
            odst, odb = outdst(ti, t)
            for sub in range(4):
                sl_ = sctr % 2
                sctr += 1
                xr, xrb = stg[sl_]
                load('sp', xr, xrb, rsrc[sub * 128:(sub + 1) * 128, :], rsb_)
                t_, tb = tt_[sl_]
                for nb in range(4):
                    pt, pb = psum[nb + 4 * (sctr % 2)]
                    for ch in range(16):
                        S.op('pe', lambda E, o=pt[:, :], l=yt[:, ch, sub * 128:(sub + 1) * 128],
                             rr_=Wb[:, ch, nb * 512:(nb + 1) * 512], st=(ch == 0), sp=(ch == 15):
                             E.matmul(o, l, rr_, start=st, stop=sp), reads=[ytb, Wbs[nb]], writes=[pb])
                    S.op('dve', lambda E, o=t_[:, nb * 512:(nb + 1) * 512], a=xr[:, nb * 512:(nb + 1) * 512], b=pt[:, :]:
                         E.scalar_tensor_tensor(out=o, in0=a, scalar=ALPHA, in1=b, op0=ALU.mult, op1=ALU.add),
                         reads=[xrb, pb], writes=[tb])
                st, stb = stats[sl_]
                m, mb = mv[sl_]
                rsd, rsb = rs[sl_]
                nmm, nmb = nm_[sl_]
                for nb in range(4):
                    S.op('dve', lambda E, o=st[:, nb, :], a=t_[:, nb * 512:(nb + 1) * 512]: E.bn_stats(out=o, in_=a),
                         reads=[tb], writes=[stb])
                S.op('dve', lambda E, o=m, a=st: E.bn_aggr(out=o, in_=a), reads=[stb], writes=[mb])
                S.op('dve', lambda E, o=rsd, a=m[:, 1:2]: E.tensor_scalar_add(out=o, in0=a, scalar1=LN_EPS),
                     reads=[mb], writes=[rsb])
                S.op('act', lambda E, o=rsd: E.activation(out=o, in_=o, func=AF.Sqrt), reads=[rsb], writes=[rsb])
                while tails:
                    tails.pop(0)()
                S.op('dve', lambda E, o=rsd: E.reciprocal(out=o, in_=o), reads=[rsb], writes=[rsb])
                S.op('dve', lambda E, o=nmm, a=m[:, 0:1], b=rsd: E.tensor_scalar(out=o, in0=a, scalar1=-1.0, scalar2=b,
                                                                               op0=ALU.mult, op1=ALU.mult),
                     reads=[mb, rsb], writes=[nmb])
                o, ob = oo[sl_]
                S.op('act', lambda E, o=o, a=t_, sc=rsd, bi=nmm: E.activation(out=o, in_=a, func=AF.Identity, bias=bi, scale=sc),
                     reads=[tb, rsb, nmb], writes=[ob])

                def tail(o=o, ob=ob, dst=odst[sub * 128:(sub + 1) * 128, :], odb=odb):
                    S.op('dve', lambda E: E.tensor_tensor(out=o, in0=o, in1=gbc, op=ALU.mult),
                         reads=[ob, gbc_b], writes=[ob])
                    S.op('dve', lambda E: E.tensor_tensor(out=o, in0=o, in1=bbc, op=ALU.add),
                         reads=[ob, bbc_b], writes=[ob])
                    store('pool', dst, odb, o, ob)
                tails.append(tail)
        while tails:
            tails.pop(0)()

    def tcols(t):
        return slice(t * 512, (t + 1) * 512)

    A_tiles = list(range(ST // 512))

    def up_A(c, ti):
        t = A_tiles[ti]
        for h in range(8):
            pt, pb = c.fm_up(h * 128, 128, ti % 2)
            c.evac_store(pt, pb, 128, KM0[h, :, tcols(t)], db["KM0"])
        c.tm(c.latnT[ti % 2][0], c.latnT[ti % 2][1], 4, c.wup, c.wup_b, 1024, 1024,
             [(VM[h, t * 512:(t + 1) * 512, :], db["VM"]) for h in range(8)], ti)

    def body_A(c, ti, t, xTt, xTb):
        c.lat_gemm(xTt, xTb, 0, ti % 2)
        pt, pb = c.fm(xTt, xTb, 512, 64)
        c.rope_store(pt, pb, 64, KM1[:, tcols(t)], db["KM1"], 1.0, ti % 2)
        if ti > 0:
            up_A(c, ti - 1)
        c.lat_finish(ti % 2)

    gemm_pass(A_tiles, xseq, None, [(w_in0, 4608, 576)], body_A, use_tm=1024,
              use_lat=(wkv_r, 2048, kvng), rope_dim=64, cs_t=cs64s, post=lambda c: up_A(c, len(A_tiles) - 1))

    def body_B1(c, ti, t, xTt, xTb):
        for h in range(8):
            pt, pb = c.fm(xTt, xTb, h * 128, 128)
            c.evac_store(pt, pb, 128, KA[h, :, tcols(t)], db["KA"])
        c.tm(xTt, xTb, 16, c.Wb, c.Wbs, 1024, 1024,
             [(VA[h, t * 512:(t + 1) * 512, :], db["VA"]) for h in range(8)], ti)

    gemm_pass(list(range(WT // 512)), xwin, None, [(w_in0, 1024, 2048)], body_B1, save_xT=XTW, use_tm=1024)

    def body_B2(c, ti, t, xTt, xTb):
        for h in range(8):
            pt, pb = c.fm(xTt, xTb, h * 128, 128)
            c.evac_store(pt, pb, 128, QA[h, :, tcols(t)], db["QA"], scale=128.0 ** -0.5)
        for h in range(8):
            pt, pb = c.fm(xTt, xTb, 1024 + h * 128, 128)
            c.evac_store(pt, pb, 128, GA[h, :, tcols(t)], db["GA"], func=AF.Silu)

    gemm_pass(W0_TILES, None, None, [(w_in0, 0, 1024), (w_in0, 3072, 1024)], body_B2, load_xT=XTW)

    def up_B3(c, ti):
        t = W0_TILES[ti]
        sc = 192.0 ** -0.5
        for h in range(8):
            pt, pb = c.fm_up(h * 128, 128, ti % 2)
            c.evac_store(pt, pb, 128, QM0[h, :, tcols(t)], db["QM0"], scale=sc)
        for h in range(8):
            pt, pb = c.fm_up(1024 + h * 64, 64, ti % 2)
            c.rope_store(pt, pb, 64, QM1[h, :, tcols(t)], db["QM1"], sc, h % 2)

    def body_B3(c, ti, t, xTt, xTb):
        c.lat_gemm(xTt, xTb, 0, ti % 2)
        for h in range(4):
            pt, pb = c.fm(xTt, xTb, 512 + h * 128, 128)
            c.evac_store(pt, pb, 128, GB[h, :, tcols(t)], db["GB"], func=AF.Silu)
        c.lat_finish(ti % 2)
        for h in range(4, 8):
            pt, pb = c.fm(xTt, xTb, 512 + h * 128, 128)
            c.evac_store(pt, pb, 128, GB[h, :, tcols(t)], db["GB"], func=AF.Silu)
        up_B3(c, ti)

    gemm_pass(W0_TILES, None, None, [(w_in0, 4096, 512), (w_in0, 5184, 1024)], body_B3, load_xT=XTW,
              use_lat=(wq_r, 1536, qng), rope_dim=64, cs_t=cs64w)

    def na_blocks(kind, R=32):
        res = []
        for ql in range(8):
            if kind == "top":
                r, base = ql, 0
                r0 = min(max(r - 4, 0), R - 8)
            elif kind == "int":
                r, base = 8 + ql, 4
                r0 = r - 4
            else:
                r, base = R - 8 + ql, R - 12
                r0 = min(max(r - 4, 0), R - 8)
            for kr in range(r0, r0 + 8):
                res.append(((kr - base) // 2, (kr - base) % 2, ql, kr - r + 7))
        return res

    na_ctx = {}

    def na_alloc(A_):
        gall, gall_b = A_.tile("gall", [120, 64], F32)
        win, winb = A_.tile("win", [64], F32)
        selt, selb = A_.tile("selt", [4], F32)
        load('sp', win, winb, nawin[:, :])
        load('sp', selt, selb, sel_in[:, :])
        load('sp', gall, gall_b, tzin.ap().rearrange("p (a b) -> p a b", a=120))
        S.op('act', lambda E: E.activation(out=gall, in_=gall, func=AF.Exp), reads=[gall_b], writes=[gall_b])
        for i in range(120):
            S.op('dve', lambda E, o=gall[:, i, :]: E.tensor_tensor(out=o, in0=o, in1=win, op=ALU.mult),
                 reads=[gall_b, winb], writes=[gall_b])
        masks = {}
        for par in range(2):
            for kind, nt_ in (("top", 6), ("int", 8), ("bot", 6), ("m3", 8), ("m6", 8)):
                for t in range(nt_):
                    m, mb = A_.tile(f"m{par}{kind}{t}", [512], BF16)
                    masks[(par, kind, t)] = (m, mb)
                    if kind in ("top", "int", "bot"):
                        S.op('dve', lambda E, m=m: E.memset(m, 0.0), writes=[mb])
        na_ctx.update(gall=gall, gall_b=gall_b, selt=selt, selb=selb)
        return masks

    def na_prep_steps(h):
        def make(masks):
            gall, gall_b = na_ctx["gall"], na_ctx["gall_b"]
            selt, selb = na_ctx["selt"], na_ctx["selb"]
            par = h % 2
            steps = []
            group = []
            for kind in ("top", "int", "bot"):
                for (t, a, ql, dr) in na_blocks(kind):
                    m, mb = masks[(par, kind, t)]

                    def cp(m=m, mb=mb, a=a, ql=ql, dr=dr):
                        S.op('dve' if na_ctx.get("force_dve") else rot(['dve', 'pool']),
                             lambda E, o=m[a * 64:(a + 1) * 64, ql * 64:(ql + 1) * 64], i=gall[a * 64:(a + 1) * 64, h * 15 + dr, :]:
                             E.tensor_copy(out=o, in_=i), reads=[gall_b], writes=[mb])
                    group.append(cp)
                    if len(group) == 8:
                        steps.append(lambda g=tuple(group): [f_() for f_ in g])
                        group = []
            if group:
                steps.append(lambda g=tuple(group): [f_() for f_ in g])
            for dst, src, toff, nsrc, s_on, s_off in (("m3", "top", 2, 6, 0, 1), ("m6", "bot", 0, 6, 2, 3)):
                for t in range(8):
                    def bl(dst=dst, src=src, toff=toff, nsrc=nsrc, s_on=s_on, s_off=s_off, t=t):
                        m, mb = masks[(par, dst, t)]
                        mi, mib = masks[(par, "int", t)]
                        S.op('dve', lambda E, sa=selt[:, s_off:s_off + 1]: E.tensor_scalar(out=m, in0=mi, scalar1=sa, scalar2=None,
                                                                                        op0=ALU.mult), reads=[mib, selb], writes=[mb])
                        ts_ = t - toff
                        if 0 <= ts_ < nsrc:
                            ms, msb = masks[(par, src, ts_)]
                            S.op('dve', lambda E, sb2=selt[:, s_on:s_on + 1]:
                                 E.scalar_tensor_tensor(out=m, in0=ms, scalar=sb2, in1=m, op0=ALU.mult, op1=ALU.add),
                                 reads=[mb, msb, selb], writes=[mb])
                    steps.append(bl)
            return steps
        return make

    na_units = []
    for h in range(8):
        par = h % 2
        qts = []
        for qt in P_W0:
            kind = "m3" if qt == 3 else ("m6" if qt == 6 else "int")
            qts.append(dict(Q=[(QA[h, :, tcols(qt)], db["QA"], 128)], G=(GA[h, :, tcols(qt)], db["GA"]),
                            keys=[(4 * qt - 2 + t, (par, kind, t), None) for t in range(8)],
                            out=(YG0[h * 128:(h + 1) * 128, tcols(qt)], db["YG0"])))
        na_units.append(dict(K=[(KA[h, :, 0:5120], db["KA"], 128)], nkt=40, V=(VA[h, 0:5120, :], db["VA"]),
                             prep_steps=na_prep_steps(h), qtiles=qts))
        qts = []
        for ql in range(4):
            wt = 10 + ql
            if ql == 0:
                keys = [(t, (par, "top", t), None) for t in range(6)]
            elif ql == 3:
                keys = [(10 + t, (par, "bot", t), None) for t in range(6)]
            else:
                keys = [(4 * ql - 2 + t, (par, "int", t), None) for t in range(8)]
            qts.append(dict(Q=[(QA[h, :, tcols(wt)], db["QA"], 128)], G=(GA[h, :, tcols(wt)], db["GA"]), keys=keys,
                            out=(YG0[h * 128:(h + 1) * 128, tcols(wt)], db["YG0"])))
        na_units.append(dict(K=[(KA[h, :, 5120:7168], db["KA"], 128)], nkt=16, V=(VA[h, 5120:7168, :], db["VA"]),
                             prep=None, qtiles=qts))
    attn_phase(na_units, na_alloc)

    mla_units = []
    for h in range(8):
        for (kq0, nkt, wts) in ((0, 64, P_W0), (8192, 16, S_WT)):
            qts = []
            for wt in wts:
                qts.append(dict(Q=[(QM0[h, :, tcols(wt)], db["QM0"], 128), (QM1[h, :, tcols(wt)], db["QM1"], 64)],
                                G=(GB[h, :, tcols(wt)], db["GB"]), keys=[(t, None, None) for t in range(nkt)],
                                out=(YG0[1024 + h * 128:1024 + (h + 1) * 128, tcols(wt)], db["YG0"])))
            mla_units.append(dict(K=[(KM0[h, :, kq0:kq0 + nkt * 128], db["KM0"], 128),
                                     (KM1[:, kq0:kq0 + nkt * 128], db["KM1"], 64)], nkt=nkt,
                                  V=(VM[h, kq0:kq0 + nkt * 128, :], db["VM"]), prep=None, qtiles=qts))
    attn_phase(mla_units, None)

    outproj_phase(YG0, wout0, ln0g, ln0b, W0_TILES,
                  lambda t: (xwin[t * 512:(t + 1) * 512, :], None),
                  lambda ti, t: (X1[t * 512:(t + 1) * 512, :], db["X1"]))

    S.barrier()
    A.reset()
    hi_t, hi_b = A.tile("hidx", [8], I32)
    load('sp', hi_t, hi_b, hidx_in[:, :])
    gwide, gwide_b = A.tile("gwide", [8, 2048], F32)

    def gfn(E):
        return E.indirect_dma_start(out=gwide.rearrange("p j d -> p (j d)"), out_offset=None,
                                    in_=X1.ap().rearrange("(r j) d -> r (j d)", j=8),
                                    in_offset=bass.IndirectOffsetOnAxis(ap=hi_t[:, 0:1], axis=0))
    S.dma('pool', gfn, gwide_b, reads=[db["X1"], hi_b], writes=[gwide_b])
    store('sp', X1H.ap().rearrange("(p j) d -> p j d", j=8), db["X1H"], gwide, gwide_b)

    def body_C1(c, ti, t, xTt, xTb):
        for h in range(16):
            pt, pb = c.fm(xTt, xTb, h * 128, 128)
            c.rope_store(pt, pb, 128, K1[h, :, tcols(t)], db["K1"], 1.0, h % 2)

    gemm_pass(W0_TILES, X1, db["X1"], [(w_in1, 2048, 2048)], body_C1, save_xT=XT1, rope_dim=128, cs_t=cs128w)

    def body_C2(c, ti, t, xTt, xTb):
        for half in range(2):
            c.tm(xTt, xTb, 16, c.Wb, c.Wbs, half * 1024, 1024,
                 [(V1[half * 8 + j, t * 512:(t + 1) * 512, :], db["V1"]) for j in range(8)], ti * 2 + half)

    gemm_pass(W0_TILES, None, None, [(w_in1, 4096, 2048)], body_C2, load_xT=XT1, use_tm=1024)

    def body_C3(c, ti, t, xTt, xTb):
        for h in range(16):
            pt, pb = c.fm(xTt, xTb, h * 128, 128)
            c.rope_store(pt, pb, 128, Q1[h, :, tcols(t)], db["Q1"], 128.0 ** -0.5, h % 2)

    SH_ROWS = {14: (X1H, 0, db["X1H"]), 15: (X1H, 512, db["X1H"])}
    gemm_pass(OUT_TILES, None, None, [(w_in1, 0, 2048)], body_C3, load_xT=XT1, rope_dim=128, cs_t=cs128w, row_tiles=SH_ROWS)

    def body_C4(c, ti, t, xTt, xTb):
        for h in range(16):
            pt, pb = c.fm(xTt, xTb, h * 128, 128)
            c.evac_store(pt, pb, 128, G1[h, :, tcols(t)], db["G1"], func=AF.Silu)

    gemm_pass(OUT_TILES, None, None, [(w_in1, 6144, 2048)], body_C4, load_xT=XT1, row_tiles=SH_ROWS)

    dil_ctx = {}

    def dil_alloc(A_):
        masks = {}
        stgs = [A_.tile(f"dstg{i}", [512], F32) for i in range(4)]
        for j in range(52):
            stg, stgb = stgs[j % 4]
            if j < 20:
                m, mb = A_.tile(f"dm{j}", [512], BF16)
                load('sp', stg, stgb, dilmask[j, :, :])
                masks[j] = (m, mb)
            else:
                m, mb = A_.tile(f"sm{j - 20}", [512], BF16)
                load('sp', stg, stgb, smask[j - 20, :, :])
                masks[('s', (j - 20) // 16, (j - 20) % 16)] = (m, mb)
            if (j // 4) % 2 == 0:
                S.op('dve', lambda E, m=m, st_=stg: E.tensor_copy(out=m, in_=st_), reads=[stgb], writes=[mb])
            else:
                S.op('act', lambda E, m=m, st_=stg: E.copy(out=m, in_=st_), reads=[stgb], writes=[mb])
        kb, kbb = A_.tile("kb", [40], F32)
        load('sp', kb, kbb, kbias[:, :])
        dil_ctx.update(kb=kb, kbb=kbb)
        return masks

    dil_units = []

    class LazyBias:
        def __init__(self, kt):
            self.kt = kt

    for h in range(16):
        qts = []
        for qt in P_OUT:
            keys = [(kt, kt - 4 * qt + 8, LazyBias(kt)) for kt in range(4 * qt - 8, 4 * qt + 12)]
            qts.append(dict(Q=[(Q1[h, :, tcols(qt)], db["Q1"], 128)], G=(G1[h, :, tcols(qt)], db["G1"]), keys=keys,
                            out=(YG1[h * 128:(h + 1) * 128, tcols(qt)], db["YG1"])))
        dil_units.append(dict(K=[(K1[h, :, 0:5120], db["K1"], 128)], nkt=40, V=(V1[h, 0:5120, :], db["V1"]),
                              prep=None, qtiles=qts))
        qts = []
        for ql in range(2):
            wt = 14 + ql
            keys = [(kt, ('s', ql, kt), None) for kt in range(16)]
            qts.append(dict(Q=[(Q1[h, :, tcols(wt)], db["Q1"], 128)], G=(G1[h, :, tcols(wt)], db["G1"]), keys=keys,
                            out=(YG1[h * 128:(h + 1) * 128, tcols(wt)], db["YG1"])))
        dil_units.append(dict(K=[(K1[h, :, 5120:7168], db["K1"], 128)], nkt=16, V=(V1[h, 5120:7168, :], db["V1"]),
                              prep=None, qtiles=qts))

    def dil_fix(masks):
        pass

    def dil_prep(masks):
        kb, kbb = dil_ctx["kb"], dil_ctx["kbb"]
        for u in dil_units:
            for q in u["qtiles"]:
                q["keys"] = [(kt, mk, ((kb[:, b.kt:b.kt + 1], kbb) if isinstance(b, LazyBias) else b))
                             for (kt, mk, b) in q["keys"]]
    dil_units[0]["prep"] = dil_prep
    attn_phase(dil_units, dil_alloc)

    outproj_phase(YG1, wout1, ln1g, ln1b, OUT_TILES,
                  lambda t: ((X1H[(t - 14) * 512:(t - 13) * 512, :], db["X1H"]) if t >= 14
                             else (X1[t * 512:(t + 1) * 512, :], db["X1"])),
                  lambda ti, t: (y_own[ti * 512:(ti + 1) * 512, :], db["y_own"]))
    S.barrier()
    S.emit()
    stack.close()
    return nc


def _dil_mult(diff):
    c = (np.abs(diff) <= 64).astype(np.float32)
    c += ((diff % 4 == 0) & (np.abs(diff) <= 256)).astype(np.float32)
    c += ((diff % 16 == 0) & (np.abs(diff) <= 1024)).astype(np.float32)
    return c


def _rope_tab(half, pos):
    inv = (10000.0 ** (-np.arange(half, dtype=np.float32) / np.float32(half))).astype(np.float32)
    ang = (pos.astype(np.float32)[None, :] * inv[:, None]).astype(np.float32)
    cos = np.cos(ang).astype(np.float32)
    sin = np.sin(ang).astype(np.float32)
    return np.ascontiguousarray(np.stack([np.concatenate([cos, cos], 0), np.concatenate([sin, sin], 0)], 0))


def _host_consts():
    c = {}
    for half in (32, 64):
        n = 2 * half
        R = np.zeros((n, n), np.float32)
        for m in range(half):
            R[m + half, m] = -1.0
            R[m, m + half] = 1.0
        c["r64" if half == 32 else "r128"] = R
    c["ident"] = np.eye(128, dtype=np.float32)
    dm = np.zeros((20, 128, 512), np.float32)
    i = np.arange(128)[:, None]
    j = np.arange(512)[None, :]
    for r in range(20):
        dm[r] = _dil_mult((r - 8) * 128 + i - j)
    c["dilmask"] = dm
    W = np.zeros((64, 64), np.float32)
    for cc in range(64):
        cs = min(max(cc - 8, 0), 48)
        W[cs:cs + 16, cc] = 1.0
    c["nawin"] = np.ascontiguousarray(np.concatenate([W, W], 0))
    c["cs64s"] = _rope_tab(32, np.concatenate([np.arange(8192), np.arange(2048)]))
    return c


_NC_CACHE = {}


def kernel(x_prompt, x_sample, ab_w_in, ab_rpb, ab_q_norm_g, ab_w_q_up, ab_kv_norm_g, ab_w_kv_up,
           ab_w_out, ab_ln_g, ab_ln_b, c_w_in, c_w_out, c_ln_g, c_ln_b):
    f = lambda a: np.ascontiguousarray(np.asarray(a, dtype=np.float32))
    xp, xs = f(x_prompt), f(x_sample)
    wq = f(ab_w_q_up).reshape(512, 8, 192)
    wkv = f(ab_w_kv_up).reshape(512, 8, 256)
    rpb = f(ab_rpb)
    shared = _host_consts()
    shared["w_in0"] = f(ab_w_in)
    shared["wq_r"] = np.ascontiguousarray(np.concatenate([wq[:, :, :128].reshape(512, 1024), wq[:, :, 128:].reshape(512, 512)], 1))
    shared["wkv_r"] = np.ascontiguousarray(np.concatenate([wkv[:, :, :128].reshape(512, 1024), wkv[:, :, 128:].reshape(512, 1024)], 1))
    shared["qng"] = np.ascontiguousarray(f(ab_q_norm_g).reshape(4, 128).T)
    shared["kvng"] = np.ascontiguousarray(f(ab_kv_norm_g).reshape(4, 128).T)
    Apad = np.zeros((8, 15, 31 + 96), np.float32)
    Apad[:, :, 48:79] = rpb
    jj = np.arange(64)[:, None] - np.arange(64)[None, :] + 63
    tz = Apad.reshape(120, 127)[:, jj]
    tz = np.ascontiguousarray(tz.transpose(1, 0, 2)).reshape(64, 120 * 64)
    shared["tzin"] = np.ascontiguousarray(np.concatenate([tz, tz], 0))
    shared["wout0"] = f(ab_w_out)
    shared["ln0g"] = f(ab_ln_g).reshape(1, D)
    shared["ln0b"] = f(ab_ln_b).reshape(1, D)
    shared["w_in1"] = f(c_w_in)
    shared["wout1"] = f(c_w_out)
    shared["ln1g"] = f(c_ln_g).reshape(1, D)
    shared["ln1b"] = f(c_ln_b).reshape(1, D)
    if "nc" not in _NC_CACHE:
        _NC_CACHE["nc"] = build_program()
    nc = _NC_CACHE["nc"]
    in_maps = []
    for c in range(NCORES):
        b, g, sb = c // 4, c % 4, c // 2
        m = dict(shared)
        m["xseq"] = np.ascontiguousarray(np.concatenate([xp[b], xs[sb]], 0))
        w0 = 2048 * g - 1536
        pos = np.arange(w0, w0 + 5120)
        valid = (pos >= 0) & (pos < 8192)
        xw = np.zeros((5120, D), np.float32)
        xw[valid] = xp[b][pos[valid]]
        m["xwin"] = np.ascontiguousarray(np.concatenate([xw, xs[sb]], 0))
        hf = c % 2
        wpos = np.concatenate([np.clip(pos, 0, 8191), np.arange(2048)])
        m["cs64w"] = _rope_tab(32, wpos)
        m["cs128w"] = _rope_tab(64, np.concatenate([wpos, np.arange(1024 * hf, 1024 * hf + 1024)]))
        hi = np.zeros((128, 8), np.int32)
        hi[:, 0] = (5120 + 1024 * hf) // 8 + np.arange(128)
        m["hidx"] = hi
        sm = np.zeros((2, 16, 128, 512), np.float32)
        for ql in range(2):
            for kt in range(16):
                rel = kt - 4 * (2 * hf + ql) + 8
                if 0 <= rel < 20:
                    sm[ql, kt] = shared["dilmask"][rel]
        m["smask"] = np.ascontiguousarray(sm.reshape(32, 128, 512))
        kb = np.where(valid, 0.0, NEG).astype(np.float32).reshape(40, 128).T
        m["kbias"] = np.ascontiguousarray(kb)
        s_top, s_bot = float(g == 0), float(g == 3)
        m["sel"] = np.ascontiguousarray(np.tile(np.array([[s_top, 1 - s_top, s_bot, 1 - s_bot]], np.float32), (128, 1)))
        in_maps.append(m)
    res = run_bass_kernel_spmd(nc, in_maps, core_ids=list(range(NCORES)))
    y_prompt = np.zeros((2, 8192, D), np.float32)
    y_sample = np.zeros((4, 2048, D), np.float32)
    for c in range(NCORES):
        b, g, sb, hf = c // 4, c % 4, c // 2, c % 2
        y = np.asarray(res.results[c]["y_own"], dtype=np.float32)
        y_prompt[b, 2048 * g:2048 * (g + 1)] = y[0:2048]
        y_sample[sb, 1024 * hf:1024 * (hf + 1)] = y[2048:3072]
    return (y_prompt, y_sample)
```
